# Optimizing a Trainium2 kernel written in Bass

```python
import math
import jax, jax.numpy as jnp
from jax import lax
import numpy as np

D_MODEL = 1024
BATCH = 4
SEQ = 4096
DEPTH = 1

GRID_W = 64
D_A = D_MODEL
A_BLOCKS = 16
A_BW = D_A // A_BLOCKS
CONV_W = 4
RG_C = 8.0
N_HEADS = 16
HEAD_DIM = 64
D_B = N_HEADS * HEAD_DIM
WIN_R = 8
WIN_C = 16
IN_SPLITS = (D_A, D_A, D_B, D_B, D_B, D_B, D_MODEL, D_MODEL)
IN_COLS = sum(IN_SPLITS)
EPS = 1e-6

kernel_name = "hybrid_rglru_natten_gated_encoder"


def rms_norm(x, g):
    x32 = x.astype(jnp.float32)
    y = x32 * lax.rsqrt(jnp.mean(x32 * x32, axis=-1, keepdims=True) + EPS)
    return (y * g.astype(jnp.float32)).astype(x.dtype)


def block_diag(x, w, b):
    B, S, C = x.shape
    xr = x.reshape(B, S, A_BLOCKS, A_BW)
    return jnp.einsum('bshi,hij->bshj', xr, w).reshape(B, S, C) + b


def centred_dwconv(x, w, b):
    C = x.shape[-1]
    y = lax.conv_general_dilated(
        x, w[:, None, :].astype(x.dtype), window_strides=(1,),
        padding=[(CONV_W // 2, CONV_W - 1 - CONV_W // 2)],
        dimension_numbers=('NWC', 'WIO', 'NWC'), feature_group_count=C)
    return y + b


def _lin_combine(e1, e2):
    a1, b1 = e1
    a2, b2 = e2
    return a1 * a2, a2 * b1 + b2


def rg_lru(x, w_r, b_r, w_i, b_i, lam, reverse):
    x32 = x.astype(jnp.float32)
    r = jax.nn.sigmoid(block_diag(x, w_r, b_r).astype(jnp.float32))
    i = jax.nn.sigmoid(block_diag(x, w_i, b_i).astype(jnp.float32))
    log_a = -RG_C * r * jax.nn.softplus(-lam.astype(jnp.float32))
    a = jnp.exp(log_a)
    u = jnp.sqrt(-jnp.expm1(2.0 * log_a)) * (i * x32)
    _, h = lax.associative_scan(_lin_combine, (a, u), axis=1, reverse=reverse)
    return h


def neighbourhood_attention(q, k, v, rpb):
    B, S, H, dh = q.shape
    rows = S // GRID_W
    win_r = min(WIN_R, rows)
    qg = q.reshape(B, rows, GRID_W, H, dh)
    kg = k.reshape(B, rows, GRID_W, H, dh)
    vg = v.reshape(B, rows, GRID_W, H, dh)
    cols = jnp.arange(GRID_W)
    c0 = jnp.clip(cols - WIN_C // 2, 0, GRID_W - WIN_C)
    cidx = c0[:, None] + jnp.arange(WIN_C)[None, :]
    dcol = cidx - cols[:, None] + (WIN_C - 1)
    scale = dh ** -0.5

    def row_fn(r):
        r0 = jnp.clip(r - win_r // 2, 0, rows - win_r)
        q_r = lax.dynamic_index_in_dim(qg, r, axis=1, keepdims=False)
        k_rows = lax.dynamic_slice_in_dim(kg, r0, win_r, axis=1)
        v_rows = lax.dynamic_slice_in_dim(vg, r0, win_r, axis=1)
        k_win = k_rows[:, :, cidx]
        v_win = v_rows[:, :, cidx]
        s = jnp.einsum('bqhd,bwqkhd->bhqwk', q_r, k_win).astype(jnp.float32) * scale
        drow = r0 + jnp.arange(win_r) - r + (WIN_R - 1)
        bias = rpb[:, drow][:, :, dcol]
        s = s + jnp.transpose(bias, (0, 2, 1, 3)).astype(jnp.float32)[None]
        p = jax.nn.softmax(s.reshape(B, H, GRID_W, win_r * WIN_C), axis=-1)
        p = p.reshape(B, H, GRID_W, win_r, WIN_C).astype(v.dtype)
        return jnp.einsum('bhqwk,bwqkhd->bqhd', p, v_win)

    out = lax.map(row_fn, jnp.arange(rows))
    return jnp.transpose(out, (1, 0, 2, 3, 4)).reshape(B, S, H * dh)


def setup_inputs(seed: int = 0) -> dict:
    key = jax.random.key(seed)
    ks = jax.random.split(key, 26)
    n = lambda k, shp, s: jax.random.normal(k, shp, jnp.float32) * s

    def lam_init(k):
        a0 = jax.random.uniform(k, (DEPTH, D_A), jnp.float32, 0.9, 0.999)
        u = a0 ** (1.0 / RG_C)
        return jnp.log(u) - jnp.log1p(-u)

    return {
        "x": n(ks[0], (BATCH, SEQ, D_MODEL), 1.0),
        "c": n(ks[1], (BATCH, D_MODEL), 1.0),
        "g_pre": 1.0 + n(ks[2], (DEPTH, D_MODEL), 0.02),
        "w_c": n(ks[3], (DEPTH, D_MODEL, 3 * D_MODEL), D_MODEL ** -0.5),
        "b_c": n(ks[4], (DEPTH, 3 * D_MODEL), 0.02),
        "w_in": n(ks[5], (DEPTH, D_MODEL, IN_COLS), D_MODEL ** -0.5),
        "conv_w": n(ks[6], (DEPTH, CONV_W, D_A), CONV_W ** -0.5),
        "conv_b": n(ks[7], (DEPTH, D_A), 0.02),
        "w_r_f": n(ks[8], (DEPTH, A_BLOCKS, A_BW, A_BW), A_BW ** -0.5),
        "b_r_f": n(ks[9], (DEPTH, D_A), 0.02),
        "w_i_f": n(ks[10], (DEPTH, A_BLOCKS, A_BW, A_BW), A_BW ** -0.5),
        "b_i_f": n(ks[11], (DEPTH, D_A), 0.02),
        "lam_f": lam_init(ks[12]),
        "w_r_b": n(ks[13], (DEPTH, A_BLOCKS, A_BW, A_BW), A_BW ** -0.5),
        "b_r_b": n(ks[14], (DEPTH, D_A), 0.02),
        "w_i_b": n(ks[15], (DEPTH, A_BLOCKS, A_BW, A_BW), A_BW ** -0.5),
        "b_i_b": n(ks[16], (DEPTH, D_A), 0.02),
        "lam_b": lam_init(ks[17]),
        "rpb": n(ks[18], (DEPTH, N_HEADS, 2 * WIN_R - 1, 2 * WIN_C - 1), 0.1),
        "w_pa": n(ks[19], (DEPTH, D_A, D_MODEL), D_A ** -0.5),
        "w_pb": n(ks[20], (DEPTH, D_B, D_MODEL), D_B ** -0.5),
        "w_o": n(ks[21], (DEPTH, D_MODEL, D_MODEL), D_MODEL ** -0.5),
        "g_final": 1.0 + n(ks[22], (D_MODEL,), 0.02),
    }


def reference(x, c, g_pre, w_c, b_c, w_in, conv_w, conv_b, w_r_f, b_r_f, w_i_f, b_i_f, lam_f,
              w_r_b, b_r_b, w_i_b, b_i_b, lam_b, rpb, w_pa, w_pb, w_o, g_final):
    B, S, D = x.shape
    split_pts = list(np.cumsum(IN_SPLITS)[:-1])
    for l in range(DEPTH):
        mod = jax.nn.silu(c) @ w_c[l] + b_c[l]
        shift, scl, gate = jnp.split(mod, 3, axis=-1)
        h = rms_norm(x, g_pre[l]) * (1.0 + scl[:, None, :]) + shift[:, None, :]

        proj = h @ w_in[l]
        xa, za, q, k, v, zb, ga, gb = jnp.split(proj, split_pts, axis=-1)

        xa = centred_dwconv(xa, conv_w[l], conv_b[l])
        h_fwd = rg_lru(xa, w_r_f[l], b_r_f[l], w_i_f[l], b_i_f[l], lam_f[l], reverse=False)
        h_bwd = rg_lru(xa, w_r_b[l], b_r_b[l], w_i_b[l], b_i_b[l], lam_b[l], reverse=True)
        y_a = (h_fwd + h_bwd).astype(x.dtype) * jax.nn.silu(za)
        y_a = y_a @ w_pa[l]

        qh = q.reshape(B, S, N_HEADS, HEAD_DIM)
        kh = k.reshape(B, S, N_HEADS, HEAD_DIM)
        vh = v.reshape(B, S, N_HEADS, HEAD_DIM)
        y_b = neighbourhood_attention(qh, kh, vh, rpb[l]) * jax.nn.silu(zb)
        y_b = y_b @ w_pb[l]

        merged = jax.nn.sigmoid(ga) * y_a + jax.nn.sigmoid(gb) * y_b
        x = x + gate[:, None, :] * (merged @ w_o[l])
    return rms_norm(x, g_final)
```

```python
import numpy as np
from contextlib import ExitStack
import concourse.bass as bass
import concourse.mybir as mybir
from concourse.bass_utils import run_bass_kernel_spmd

F32 = mybir.dt.float32
BF16 = mybir.dt.bfloat16
AF = mybir.ActivationFunctionType
ALU = mybir.AluOpType

COMPUTE = ("pe", "act", "dve", "pool")
QUEUES = ("sp",) + COMPUTE


class Planner:
    def __init__(self, n_dma_sems=24):
        self.ops = {e: [] for e in QUEUES}
        self.count = {e: 0 for e in COMPUTE}
        self.waited = {e: {} for e in QUEUES}
        self.state = {}
        self.n_dma = n_dma_sems
        self.dma_i = 0
        self.dma_q = [0, 0]
        self.dma_cnt = [0] * n_dma_sems
        self.dma_last = [None] * n_dma_sems
        self.all_tickets = {}
        self.pe_mode = None
        self.pe_unsig = False

    def _deps(self, eng, reads, writes, extra=()):
        deps = {}

        def add(t):
            if t is None:
                return
            k, v = t
            if deps.get(k, 0) < v:
                deps[k] = v

        for k in reads:
            st = self.state.get(k)
            if st:
                add(st[0])
        for k in writes:
            st = self.state.get(k)
            if st:
                add(st[0])
                for t in st[1]:
                    add(t)
        for t in extra:
            add(t)
        waits = []
        for k, v in deps.items():
            if eng == "pe" and k == "pe":
                continue
            if self.waited[eng].get(k, 0) >= v:
                continue
            self.waited[eng][k] = v
            waits.append((k, v))
        return waits

    def _commit(self, ticket, reads, writes):
        for k in reads:
            st = self.state.setdefault(k, [None, []])
            st[1].append(ticket)
        for k in writes:
            self.state[k] = [ticket, []]
        self.all_tickets[ticket[0]] = max(self.all_tickets.get(ticket[0], 0), ticket[1])

    def op(self, eng, fn, reads=(), writes=(), signal=True, mode="mm"):
        waits = self._deps(eng, reads, writes)
        if eng == "pe":
            if self.pe_mode not in (None, mode):
                assert not self.pe_unsig
                v = self.count["pe"]
                if self.waited["pe"].get("pe", 0) < v:
                    self.waited["pe"]["pe"] = v
                    waits.append(("pe", v))
            self.pe_mode = mode
            self.pe_unsig = not signal
        if signal:
            self.count[eng] += 1
            ticket = (eng, self.count[eng])
            inc = (eng, 1)
        else:
            ticket = (eng, self.count[eng] + 1)
            inc = None
        self._commit(ticket, reads, writes)
        self.ops[eng].append((waits, fn, inc))
        return ticket

    def dma(self, queue, fn, reads=(), writes=()):
        half = self.n_dma // 2
        qi = 1 if queue == "pool" else 0
        i = qi * half + self.dma_q[qi] % half
        self.dma_q[qi] += 1
        key = ("dma", i)
        extra = [self.dma_last[i]] if self.dma_last[i] else []
        waits = self._deps(queue, reads, writes, extra)
        self.dma_cnt[i] += 16
        ticket = (key, self.dma_cnt[i])
        self.dma_last[i] = ticket
        self._commit(ticket, reads, writes)
        self.ops[queue].append((waits, fn, (key, 16)))
        return ticket

    def barrier(self):
        assert not self.pe_unsig
        for eng in QUEUES:
            waits = []
            for k, v in self.all_tickets.items():
                if self.waited[eng].get(k, 0) >= v:
                    continue
                self.waited[eng][k] = v
                waits.append((k, v))
            if waits:
                self.ops[eng].append((waits, None, None))
        self.state = {}

    def final_wait(self, eng="sp"):
        waits = []
        for k, v in self.all_tickets.items():
            if self.waited[eng].get(k, 0) >= v:
                continue
            self.waited[eng][k] = v
            waits.append((k, v))
        self.ops[eng].append((waits, None, None))

    def emit(self, nc, stack):
        sems = {}
        for e in COMPUTE:
            sems[e] = stack.enter_context(nc.semaphore("s_" + e))
        for i in range(self.n_dma):
            sems[("dma", i)] = stack.enter_context(nc.semaphore("s_dma%d" % i))
        block = stack.enter_context(nc.Block())

        def run(engname):
            def body(eng):
                for waits, fn, inc in self.ops[engname]:
                    for k, v in waits:
                        eng.wait_ge(sems[k], v)
                    if fn is None:
                        continue
                    ins = fn(eng)
                    if inc is not None:
                        ins.then_inc(sems[inc[0]], inc[1])
            return body

        block.sync(run("sp"))
        block.tensor(run("pe"))
        block.scalar(run("act"))
        block.vector(run("dve"))
        block.gpsimd(run("pool"))


D = 1024
NT = 4096
OWN = 2048
KV0 = 1792
NKV = NT - KV0
PS0 = 1920
NEG = -30000.0
EPS = 1e-6
ARENA_BYTES = 204 * 1024


def build_nc():
    nc = bass.Bass("TRN2", target_bir_lowering=False)

    def din(name, shape):
        return nc.dram_tensor(name, list(shape), F32, kind="ExternalInput").ap()

    xloc = din("xloc", [NT, D])
    ccol = din("ccol", [128, 8])
    wc_t = din("wc_t", [12, 128, 8 * 256])
    bc_rep = din("bc_rep", [128, 3072])
    gpre_rep = din("gpre_rep", [128, D])
    gfin_rep = din("gfin_rep", [128, D])
    win_t = din("win_t", [64, 128, 8 * 128])
    wv_t = din("wv_t", [4, 128, 8 * 256])
    wpa_t = din("wpa_t", [8, 128, 8 * 128])
    wpb_t = din("wpb_t", [8, 128, 8 * 128])
    wo_t = din("wo_t", [128, 8 * D])
    dconv = din("dconv", [128, 8 * 5 * 128])
    gmat = din("gmat", [128, 8 * 4 * 128])
    chvec = din("chvec", [128, 7 * 8])
    bt = din("bt", [8, 128, 3 * 2 * 640])
    ident = din("ident", [128, 128])
    out = nc.dram_tensor("out", [OWN, D], F32, kind="ExternalOutput").ap()

    P = Planner()
    with ExitStack() as st:
        arena = st.enter_context(nc.sbuf_tensor("arena", [128, ARENA_BYTES // 2], BF16))
        a16 = arena
        a32 = arena.bitcast(F32)
        psall = st.enter_context(nc.psum_tensor("psall", [128, 8 * 512], F32))
        psall16 = psall.bitcast(BF16)
        banks = [psall[:, i * 512:(i + 1) * 512] for i in range(8)]
        banks16 = [psall16[:, i * 1024:(i + 1) * 1024] for i in range(8)]

        def V16(off, n):
            assert off % 2 == 0 and off + 2 * n <= ARENA_BYTES, (off, n)
            return a16[:, off // 2: off // 2 + n]

        def V32(off, n):
            assert off % 4 == 0 and off + 4 * n <= ARENA_BYTES, (off, n)
            return a32[:, off // 4: off // 4 + n]

        def r3(ap, a):
            return ap.rearrange("p (a b) -> p a b", a=a)

        R_HOWN = 0
        R_YA = R_HOWN + 8 * NKV * 2
        R_YB = R_YA + 8 * OWN * 2
        R_CONST = R_YB + 8 * OWN * 2
        o = R_CONST
        ident_b = V16(o, 128); o += 256
        ones_b = V16(o, 128); o += 256
        chv = r3(V32(o, 56), 7); o += 224
        der = r3(V32(o, 80), 10); o += 320
        cols = V32(o, 8); o += 32
        carry = V32(o, 8); o += 32
        ss = V32(o, 32); o += 128
        lnv = V32(o, 32); o += 128
        rstd = V32(o, 32); o += 128
        dconv_b = V16(o, 5120); o += 10240
        gmat_b = V16(o, 4096); o += 8192
        G_bc = V32(o, D); o += 4096
        gfin = V32(o, D); o += 4096
        R_TMP = (o + 63) // 64 * 64
        TMP_BYTES = ARENA_BYTES - R_TMP

        hT_own = r3(V16(R_HOWN, 8 * NKV), 8)
        yaT = r3(V16(R_YA, 8 * OWN), 8)
        ybT = r3(V16(R_YB, 8 * OWN), 8)
        hT_oth = r3(V16(R_YB, 8 * OWN), 8)

        bank_rr = [0]

        def nb():
            b = bank_rr[0] % 8
            bank_rr[0] += 1
            return b

        def nb2():
            if bank_rr[0] % 2:
                bank_rr[0] += 1
            b = bank_rr[0] % 8
            bank_rr[0] += 2
            return b

        def BK(b):
            return ("ps", b)

        def mm_group(bank, out_ap, pairs, reads, mode="mm"):
            n = len(pairs)
            for i, (l, r) in enumerate(pairs):
                P.op("pe", (lambda l=l, r=r, s=(i == 0), e_=(i == n - 1):
                            lambda e: e.matmul(out_ap, l, r, start=s, stop=e_))(),
                     reads=reads, writes=[BK(bank)], signal=(i == n - 1), mode=mode)

        def act(out_ap, in_ap, func, reads, writes, bias=None, scale=None, accum=None):
            kw = {}
            if bias is not None:
                kw["bias"] = bias
            if scale is not None:
                kw["scale"] = scale
            if accum is not None:
                kw["accum_out"] = accum
            P.op("act", lambda e: e.activation(out_ap, in_ap, func, **kw), reads=reads, writes=writes)

        def stt(out_ap, in0, scalar, in1, op0, op1, reads, writes, eng="dve"):
            P.op(eng, lambda e: e.scalar_tensor_tensor(out_ap, in0, scalar, in1, op0, op1), reads=reads, writes=writes)

        def tt(out_ap, in0, in1, op, reads, writes, eng="dve"):
            P.op(eng, lambda e: e.tensor_tensor(out_ap, in0, in1, op), reads=reads, writes=writes)

        def ts(out_ap, in0, s1, s2, op0, op1, reads, writes, eng="dve"):
            P.op(eng, lambda e: e.tensor_scalar(out_ap, in0, s1, s2, op0, op1), reads=reads, writes=writes)

        def cp(out_ap, in_ap, reads, writes, eng="dve"):
            P.op(eng, lambda e: e.tensor_copy(out_ap, in_ap), reads=reads, writes=writes)

        def ms(ap, val, writes, eng="dve"):
            P.op(eng, lambda e: e.memset(ap, val), writes=writes)

        def ld(out_ap, in_ap, writes, q="sp", reads=()):
            P.dma(q, lambda e: e.dma_start(out=out_ap, in_=in_ap), reads=reads, writes=writes)

        def ldc(out_ap, in_ap, writes, reads=()):
            P.dma("pool", lambda e: e.dma_start(out=out_ap, in_=in_ap, max_dma_last_dim=4096), reads=reads, writes=writes)

        t = R_TMP
        NWS = 3
        wcs = [V32(t + i * 8192, 2048) for i in range(NWS)]; t += NWS * 8192
        wcb = [r3(V16(t + i * 4096, 2048), 8) for i in range(2)]; t += 8192
        sc_rep = r3(V16(t, 1024), 8); t += 2048
        bcr = V32(t, 3072); t += 12288
        gpr = V32(t, D); t += 4096
        cc = V32(t, 8); t += 32
        tz0 = V32(t, 8); t += 32
        s2 = V32(t, 8); t += 32
        ev = V32(t, 16); t += 64
        assert t - R_TMP <= TMP_BYTES
        NXT = 4
        xt = [V32(t + i * 4096, D) for i in range(NXT)]; t += NXT * 4096
        junk = V16(t, D); t += 2048
        t1 = V32(t, D); t += 4096
        hb = [V16(t + i * 2048, D) for i in range(2)]; t += 4096
        assert t - R_TMP <= TMP_BYTES
        p1bank = {}

        def p1A(tk):
            x3 = tk % NXT
            ld(xt[x3], xloc[tk * 128:(tk + 1) * 128, :], [("xt", x3)])
            act(junk, xt[x3], AF.Square, [("xt", x3)], ["junk", ("ss", tk)], accum=ss[:, tk:tk + 1])
            act(lnv[:, tk:tk + 1], ss[:, tk:tk + 1], AF.Ln, [("ss", tk), "cols"], [("lnv", tk)], bias=cols[:, 1:2], scale=1.0 / D)
            act(rstd[:, tk:tk + 1], lnv[:, tk:tk + 1], AF.Exp, [("lnv", tk)], [("rstd", tk)], scale=-0.5)

        mod_bc = V32(R_YA, 3072)
        A_bc = V32(R_YA + 12288, D)
        B_bc = mod_bc[:, 0:D]

        ld(cc, ccol, ["cc"])
        ld(chv.rearrange("p a b -> p (a b)"), chvec, ["chv"])
        for nbk in range(NWS):
            ld(wcs[nbk], wc_t[nbk], [("wcs", nbk)], q="sp")
        ld(bcr, bc_rep, ["bcr"])
        ld(gpr, gpre_rep, ["gpr"])
        ldc(ident_b, ident, ["ident"])
        ms(ones_b, 1.0, ["ones"])
        ms(cols[:, 0:1], 1.0, ["cols"])
        ms(cols[:, 1:2], EPS, ["cols"])
        ms(carry, 0.0, ["carry"])
        dcv = r3(dconv_b, 40)
        gmv = r3(gmat_b, 32)

        act(tz0, cc, AF.Tanh, ["cc"], ["tz0"], scale=0.5)
        stt(s2, tz0, 1.0, cc, ALU.add, ALU.mult, ["tz0", "cc"], ["s2"])
        for kt in range(8):
            ts(sc_rep[:, kt, :], ones_b, s2[:, kt:kt + 1], 0.5, ALU.mult, ALU.mult, ["ones", "s2"], [("scr", kt)])
        for nbk in range(12):
            w = wcb[nbk % 2]
            ws_ = nbk % NWS
            act(w.rearrange("p a b -> p (a b)"), wcs[ws_], AF.Copy, [("wcs", ws_)], [("wcb", nbk % 2)])
            if nbk + NWS < 12:
                ld(wcs[ws_], wc_t[nbk + NWS], [("wcs", ws_)], q="sp")
            b = nb()
            mm_group(b, banks[b][:, 0:256], [(sc_rep[:, kt, :], w[:, kt, :]) for kt in range(8)],
                     [("wcb", nbk % 2)] + [("scr", kt) for kt in range(8)])
            tt(mod_bc[:, nbk * 256:(nbk + 1) * 256], banks[b][:, 0:256], bcr[:, nbk * 256:(nbk + 1) * 256], ALU.add,
               ["bcr"], [BK(b), ("mod", nbk)])
            if nbk == 7:
                for tk0_ in range(NXT):
                    p1A(tk0_)
        stt(A_bc, mod_bc[:, D:2 * D], 1.0, gpr, ALU.add, ALU.mult, [("mod", k) for k in range(4, 8)] + ["gpr"], ["A_bc"])
        cp(G_bc, mod_bc[:, 2 * D:3 * D], [("mod", k) for k in range(8, 12)], ["G_bc"])
        ld(gfin, gfin_rep, ["gfin"])
        for g in range(4):
            ts(der[:, g, :], chv[:, 1 + g, :], 0.5, None, ALU.mult, ALU.bypass, ["chv"], [("der", g)])
        act(ev, chv[:, 5:7, :].rearrange("p a b -> p (a b)"), AF.Exp, ["chv"], ["ev"], scale=-1.0)
        act(ev, ev, AF.Ln, ["ev", "cols"], ["ev"], bias=cols[:, 0:1], scale=1.0)
        for q in range(2):
            ts(der[:, 4 + 2 * q, :], ev[:, q * 8:(q + 1) * 8], -8.0, None, ALU.mult, ALU.bypass, ["ev"], [("der", 4 + 2 * q)])
            ts(der[:, 5 + 2 * q, :], ev[:, q * 8:(q + 1) * 8], -4.0, None, ALU.mult, ALU.bypass, ["ev"], [("der", 5 + 2 * q)])

        def p1B(tk):
            x3 = tk % NXT
            bsel = tk % 2
            stt(t1, xt[x3], rstd[:, tk:tk + 1], A_bc, ALU.mult, ALU.mult, [("xt", x3), ("rstd", tk), "A_bc"], ["t1"])
            tt(hb[bsel], t1, B_bc, ALU.add, ["t1"] + [("mod", k) for k in range(4)], [("hb", bsel)])
            b = nb()
            p1bank[tk] = b
            for kt in range(8):
                P.op("pe", (lambda kt=kt, b=b, bsel=bsel: lambda e: e.transpose(
                    banks16[b][:, kt * 128:(kt + 1) * 128], hb[bsel][:, kt * 128:(kt + 1) * 128], ident_b))(),
                    reads=[("hb", bsel), "ident"], writes=[BK(b)], signal=(kt == 7), mode="tr")

        def p1C(tk):
            b = p1bank[tk]
            src = r3(banks16[b][:, :], 8)
            if tk < 16:
                act(hT_oth[:, :, tk * 128:(tk + 1) * 128], src, AF.Copy, [], [BK(b), ("hoth", tk)])
            if tk >= 16:
                j = tk - 14
                act(hT_own[:, :, j * 128:(j + 1) * 128], src, AF.Copy, [], [BK(b), ("hown", j)])
            elif tk >= 14:
                j = tk - 14
                cp(hT_own[:, :, j * 128:(j + 1) * 128], src, [], [BK(b), ("hown", j)])

        for s_ in range(34):
            if NXT <= s_ < 32:
                p1A(s_)
            if s_ == 10:
                ldc(dconv_b, dconv, ["dconv"], reads=[("xt", 10 % NXT)])
                ldc(gmat_b, gmat, ["gmat"], reads=[("xt", 10 % NXT)])
            if 0 <= s_ - 1 < 32:
                p1B(s_ - 1)
            if 0 <= s_ - 2 < 32:
                p1C(s_ - 2)
        P.barrier()

        def branchA(part):
            own = (part == 1)
            t = R_TMP
            WP = 2176 if own else 2048
            SEG = []
            for q_ in range(2 if own else 1):
                wseg = WP if q_ == 0 else OWN
                SEG.append(dict(TR=V32(t, wseg), TI=V32(t + wseg * 4, wseg), A2=V32(t + wseg * 8, wseg), U=V32(t + wseg * 12, wseg)))
                t += wseg * 16
            if not own:
                SEG0_alt = dict(TR=V32(t, WP), TI=V32(t + WP * 4, WP)); t += WP * 8
                wA2 = [r3(V16(t + i * 2048, 1024), 8) for i in range(2)]; t += 4096
                wA = [wA2[0], None, wA2[1], None]
            else:
                wA = [r3(V16(t + i * 2048, 1024), 8) for i in range(4)]; t += 8192
            if own:
                assert t - R_TMP <= TMP_BYTES, t - R_TMP
                t = R_YB
                lim = R_YB + 8 * OWN * 2
            else:
                lim = R_TMP + TMP_BYTES
            NX = NKV if own else 2048
            xa_bf = V16(t, NX + 4); t += (NX + 4) * 2
            t = (t + 3) // 4 * 4
            xc_f = V32(t, NX); t += NX * 4
            if not own:
                xc_f2 = V32(t, NX); t += NX * 4
            xc_b = V16(t, NX); t += NX * 2
            t = (t + 3) // 4 * 4
            if own:
                HP = V32(t, 2176); t += 2176 * 4
                NTZ = 5
                tzt = [V32(t + i * 1024, 256) for i in range(NTZ)]; t += NTZ * 1024
            assert t <= lim, (t, lim)
            hsrc = hT_own if own else hT_oth
            xblocks = [(i * 512, 512) for i in range(NX // 512)] + ([(2048, 256)] if own else [])
            nxb = len(xblocks)
            xblocks_i = [(i, c0, w_) for i, (c0, w_) in enumerate(xblocks)]
            ms(xa_bf[:, 0:2], 0.0, [("xa", "padl")])
            ms(xa_bf[:, NX + 2:NX + 4], 0.0, [("xa", "padr")])

            def loadw(ct):
                ldc(wA[(ct % 2) * 2].rearrange("p a b -> p (a b)"), win_t[ct], [("wA", (ct % 2) * 2)])
                if own:
                    ldc(wA[(ct % 2) * 2 + 1].rearrange("p a b -> p (a b)"), win_t[8 + ct], [("wA", (ct % 2) * 2 + 1)])

            def segbuf(q_, ct):
                S_ = SEG[q_]
                if (not own) and ct % 2 == 1:
                    S_ = dict(S_, TR=SEG0_alt["TR"], TI=SEG0_alt["TI"])
                return S_

            def sk(q_, gi, ct):
                if (not own) and gi in (0, 1):
                    return ("seg", q_, gi, ct % 2)
                return ("seg", q_, gi)

            loadw(0)
            segs = [(0, 0, (PS0 - KV0) if own else 0, 2176 if own else PS0)]
            if own:
                segs.append((1, 2, 256, OWN))
            def XCF(ct):
                return xc_f if (own or ct % 2 == 0) else xc_f2

            def xfkey(ct, bi):
                return ("xcf", bi) if own else ("xcf", ct % 2, bi)

            def stage_F(ct):
                wx = wA[(ct % 2) * 2]
                wz = wA[(ct % 2) * 2 + 1]
                if ct + 1 < 8:
                    loadw(ct + 1)
                pairs = [xblocks_i[i:i + 2] for i in range(0, nxb, 2)]
                for pr in pairs:
                    b = nb2()
                    for k_, (bi, c0, w_) in enumerate(pr):
                        mm_group(b + k_, banks[b + k_][:, 0:w_], [(wx[:, kt, :], hsrc[:, kt, c0:c0 + w_]) for kt in range(8)],
                                 [("wA", (ct % 2) * 2)])
                    c0 = pr[0][1]
                    W = sum(p_[2] for p_ in pr)
                    act(xa_bf[:, 2 + c0:2 + c0 + W], psall[:, b * 512:b * 512 + W], AF.Copy, [],
                        [BK(b + k_) for k_ in range(len(pr))] + [("xa", p_[0]) for p_ in pr])
                for pr in pairs:
                    b = nb2()
                    for k_, (bi, c0, w_) in enumerate(pr):
                        rk = [("xa", k) for k in range(max(0, bi - 1), min(nxb, bi + 2))] + [("xa", "padl"), ("xa", "padr"), "dconv"]
                        mm_group(b + k_, banks[b + k_][:, 0:w_], [(dcv[:, ct * 5 + o_, :], xa_bf[:, c0 + o_:c0 + o_ + w_]) for o_ in range(5)], rk)
                    c0 = pr[0][1]
                    W = sum(p_[2] for p_ in pr)
                    bks = [BK(b + k_) for k_ in range(len(pr))]
                    src_ = psall[:, b * 512:b * 512 + W]
                    act(xc_b[:, c0:c0 + W], src_, AF.Identity, ["chv"], bks + [("xcb", p_[0]) for p_ in pr], bias=chv[:, 0, ct:ct + 1])
                    if own:
                        act(XCF(ct)[:, c0:c0 + W], src_, AF.Identity, ["chv"], bks + [xfkey(ct, p_[0]) for p_ in pr], bias=chv[:, 0, ct:ct + 1])
                    else:
                        ts(XCF(ct)[:, c0:c0 + W], src_, chv[:, 0, ct:ct + 1], None, ALU.add, ALU.bypass, ["chv"], bks + [xfkey(ct, p_[0]) for p_ in pr])
                if own:
                    for bi in range(8):
                        c0 = 256 + bi * 256
                        b = nb()
                        tz = tzt[(ct * 8 + bi) % NTZ]
                        tk_ = ("tzt", (ct * 8 + bi) % NTZ)
                        mm_group(b, banks[b][:, 0:256], [(wz[:, kt, :], hT_own[:, kt, c0:c0 + 256]) for kt in range(8)],
                                 [("wA", (ct % 2) * 2 + 1)])
                        act(tz, banks[b][:, 0:256], AF.Tanh, [], [BK(b), tk_], scale=0.5)
                        stt(yaT[:, ct, bi * 256:(bi + 1) * 256], tz, 1.0, banks[b][:, 0:256], ALU.add, ALU.mult, [tk_], [BK(b), ("sza", ct, bi)])

            def stage_G1(ct):
                for (q_, g0, s0, wd) in segs:
                    S_ = segbuf(q_, ct)
                    c = 0
                    while c < wd:
                        w1 = min(512, wd - c)
                        w2 = min(512, wd - c - w1)
                        w_ = w1 + w2
                        xk = [("xcb", k) for k in range((s0 + c) // 512, min(nxb, (s0 + c + w_ - 1) // 512 + 1))]
                        for gi, dst in ((0, S_["TR"]), (1, S_["TI"])):
                            b = nb2()
                            mm_group(b, banks[b][:, 0:w1], [(gmv[:, ct * 4 + g0 + gi, :], xc_b[:, s0 + c:s0 + c + w1])], xk + ["gmat"])
                            bks = [BK(b)]
                            if w2 > 0:
                                mm_group(b + 1, banks[b + 1][:, 0:w2], [(gmv[:, ct * 4 + g0 + gi, :], xc_b[:, s0 + c + w1:s0 + c + w_])], xk + ["gmat"])
                                bks.append(BK(b + 1))
                            act(dst[:, c:c + w_], psall[:, b * 512:b * 512 + w_], AF.Tanh, [("der", g0 + gi)], bks + [sk(q_, gi, ct)],
                                bias=der[:, g0 + gi, ct:ct + 1], scale=0.5)
                        c += w_
                    hcP = der[:, 5 + g0, ct:ct + 1]
                    act(S_["TR"][:, 0:wd], S_["TR"][:, 0:wd], AF.Exp, [sk(q_, 0, ct), ("der", 5 + g0)], [sk(q_, 0, ct)], bias=hcP, scale=hcP)
                    tt(S_["A2"][:, 0:wd], S_["TR"][:, 0:wd], S_["TR"][:, 0:wd], ALU.mult, [sk(q_, 0, ct)], [("seg", q_, 2)], eng="pool")

            def stage_G2(ct):
                for (q_, g0, s0, wd) in segs:
                    S_ = segbuf(q_, ct)
                    act(S_["A2"][:, 0:wd], S_["A2"][:, 0:wd], AF.Sqrt, [("seg", q_, 2), "cols"], [("seg", q_, 2)], bias=cols[:, 0:1], scale=-1.0)

            def stage_D(ct):
                for (q_, g0, s0, wd) in segs:
                    S_ = segbuf(q_, ct)
                    TR, TI, A2, U = S_["TR"], S_["TI"], S_["A2"], S_["U"]
                    xfk = [xfkey(ct, k) for k in range(s0 // 512, min(nxb, (s0 + wd - 1) // 512 + 1))]
                    stt(U[:, 0:wd], TI[:, 0:wd], 1.0, XCF(ct)[:, s0:s0 + wd], ALU.add, ALU.mult, [sk(q_, 1, ct)] + xfk, [("seg", q_, 3)])
                    stt(U[:, 0:wd], U[:, 0:wd], 0.5, A2[:, 0:wd], ALU.mult, ALU.mult, [("seg", q_, 3), ("seg", q_, 2)], [("seg", q_, 3)])
                    if not own:
                        P.op("dve", (lambda TI=TI, TR=TR, U=U, wd=wd: lambda e: e.tensor_tensor_scan(TI[:, 0:wd], TR[:, 0:wd], U[:, 0:wd], 0.0, ALU.mult, ALU.add))(),
                             reads=[sk(q_, 0, ct), ("seg", q_, 3)], writes=[sk(q_, 1, ct)])
                        cp(carry[:, ct:ct + 1], TI[:, wd - 1:wd], [sk(q_, 1, ct)], [("carry", ct)])
                    elif q_ == 0:
                        P.op("dve", (lambda TR=TR, U=U, wd=wd, ct=ct: lambda e: e.tensor_tensor_scan(HP[:, 0:wd], TR[:, 0:wd], U[:, 0:wd], carry[:, ct:ct + 1], ALU.mult, ALU.add))(),
                             reads=[sk(q_, 0, ct), ("seg", q_, 3), ("carry", ct)], writes=["HP"])
                    else:
                        P.op("dve", (lambda A2=A2, TR=TR, U=U, wd=wd: lambda e: e.tensor_tensor_scan(A2[:, 0:wd][:, ::-1], TR[:, 0:wd][:, ::-1], U[:, 0:wd][:, ::-1], 0.0, ALU.mult, ALU.add))(),
                             reads=[("seg", q_, 0), ("seg", q_, 3)], writes=[("seg", q_, 2)])
                        tt(HP[:, 128:2176], HP[:, 128:2176], A2[:, 0:OWN], ALU.add, ["HP", ("seg", q_, 2)], ["HP"], eng="pool")
                        tt(yaT[:, ct, :], HP[:, 128:2176], yaT[:, ct, :], ALU.mult, ["HP"] + [("sza", ct, k) for k in range(8)], [("ya", ct)], eng="pool")

            stage_F(0)
            for ct in range(8):
                stage_G1(ct)
                if (not own) and ct + 1 < 8:
                    stage_F(ct + 1)
                stage_G2(ct)
                stage_D(ct)
                if own and ct + 1 < 8:
                    stage_F(ct + 1)
            P.barrier()

        branchA(0)
        branchA(1)

        t = R_TMP
        wvb = r3(V16(t, 2048), 8); t += 4096
        Vb = r3(V16(t, 18 * 256), 18); t += 9216
        wq = [r3(V16(t + i * 2048, 1024), 8) for i in range(3)]; t += 6144
        btb = V16(t, 3840); t += 7680
        qz = [V16(t + i * 4096, OWN) for i in range(2)]; t += 8192
        kT = V16(t, NKV); t += NKV * 2
        szb = V16(t, OWN); t += 4096
        Lg = [V32(t + i * 5120, 1280) for i in range(2)]; t += 10240
        PT = [V16(t + i * 2560, 1280) for i in range(3)]; t += 7680
        lnd = [V32(t + i * 512, 128) for i in range(2)]; t += 1024
        rden = [V32(t + i * 512, 128) for i in range(2)]; t += 1024
        tn = [V32(t + i * 512, 128) for i in range(2)]; t += 1024
        tzbs = [V32(t + i * 2048, 512) for i in range(2)]; t += 4096
        assert t - R_TMP <= TMP_BYTES, t - R_TMP
        ms(qz[0][64:128, :], 0.0, [("qzz", 0)])
        ms(qz[1][0:64, :], 0.0, [("qzz", 1)])
        kvblocks = [(i * 512, 512) for i in range(4)] + [(2048, 256)]
        SB = [[0, 1, 2], [3, 4, 5]]
        NDB = [6, 7]
        step = [0]

        def att_S(hp, hpl, pi):
            i = step[0]
            step[0] += 1
            sl = i % 2
            r_loc = 32 + 2 * pi
            R0 = min(max(r_loc - 4, 0), 54)
            koff = (R0 - 28) * 64
            qoff = pi * 128
            var = 0 if pi <= 13 else pi - 13
            kkeys = [("kT", k) for k in range(koff // 512, (koff + 639) // 512 + 1)]
            B = SB[sl]
            for e_ in range(2):
                for j in range(5):
                    if j < 4:
                        bb, oc = B[e_], j * 128
                    else:
                        bb, oc = B[2], e_ * 128
                    for kh in range(2):
                        P.op("pe", (lambda bb=bb, oc=oc, kh=kh, j=j, e_=e_: lambda e: e.matmul(
                            banks[bb][kh * 64:(kh + 1) * 64, oc:oc + 128],
                            kT[:, koff + 128 * j + 64 * kh:koff + 128 * j + 64 * (kh + 1)],
                            qz[e_][:, qoff:qoff + 128], start=True, stop=False))(),
                            reads=kkeys + [("qz", e_, pi // 4), ("qzz", e_)], writes=[BK(bb)], signal=False, mode="ct")
                        bcol = ((var * 2 + e_) * 5 + j) * 128 + kh * 64
                        P.op("pe", (lambda bb=bb, oc=oc, kh=kh, bcol=bcol: lambda e: e.matmul(
                            banks[bb][kh * 64:(kh + 1) * 64, oc:oc + 128],
                            btb[:, bcol:bcol + 64], ident_b, start=False, stop=True))(),
                            reads=["btb", "ident"], writes=[BK(bb)], signal=(kh == 1), mode="ct")
            return dict(hp=hp, hpl=hpl, pi=pi, sl=sl, tk0=koff // 128, qoff=qoff, p3=i % 3, var=var)

        def att_L(d):
            pass

        def att_X(d):
            sl, p3 = d["sl"], d["p3"]
            B = SB[sl]
            for e_ in range(2):
                act(PT[p3][:, e_ * 640:e_ * 640 + 512], banks[B[e_]][:, :], AF.Exp, [], [BK(B[e_]), ("PT", p3, e_)])
            act(r3(PT[p3], 2)[:, :, 512:640], r3(banks[B[2]][:, 0:256], 2), AF.Exp, [], [BK(B[2]), ("PT", p3, 2)])

        def att_ND(d):
            sl, p3, tk0, hpl = d["sl"], d["p3"], d["tk0"], d["hpl"]
            bn_ = NDB[sl]
            vk = [("V", tk0 + j) for j in range(5)]
            for which in range(2):
                for j in range(5):
                    for e_ in range(2):
                        hs = slice(e_ * 64, (e_ + 1) * 64)
                        rhs_ = PT[p3][:, e_ * 640 + j * 128:e_ * 640 + (j + 1) * 128]
                        if which == 0:
                            lhs_ = Vb[:, tk0 + j, hpl * 128 + e_ * 64:hpl * 128 + (e_ + 1) * 64]
                            out_ = banks[bn_][hs, 0:128]
                        else:
                            lhs_ = ones_b[:, 0:64]
                            out_ = banks[bn_][hs, 128:256]
                        P.op("pe", (lambda out_=out_, lhs_=lhs_, rhs_=rhs_, j=j: lambda e: e.matmul(
                            out_, lhs_, rhs_, start=(j == 0), stop=(j == 4)))(),
                            reads=[("PT", p3, 0), ("PT", p3, 1), ("PT", p3, 2), "ones"] + vk, writes=[BK(bn_)], signal=(j == 4 and e_ == 1), mode="ct")

        def att_R(d):
            sl = d["sl"]
            bn_ = NDB[sl]
            P.op("dve", (lambda sl=sl, bn_=bn_: lambda e: e.reciprocal(rden[sl], banks[bn_][:, 128:256]))(),
                 reads=[], writes=[BK(bn_), ("rden", sl)])

        def att_E(d):
            sl, hp, pi, qoff = d["sl"], d["hp"], d["pi"], d["qoff"]
            bn_ = NDB[sl]
            tt(tn[sl], banks[bn_][:, 0:128], rden[sl], ALU.mult, [("rden", sl)], [BK(bn_), ("tn", sl)])
            tt(ybT[:, hp, qoff:qoff + 128], tn[sl], szb[:, qoff:qoff + 128], ALU.mult,
               [("tn", sl), ("szb", pi // 4)], [("yb", hp, pi)])

        pend = []

        def att_flush():
            while pend:
                old_ = pend.pop(0)
                att_ND(old_)
                att_R(old_)
                att_E(old_)

        for hq in range(4):
            att_flush()
            ldc(wvb.rearrange("p a b -> p (a b)"), wv_t[hq], ["wvb"])
            for tk in range(18):
                b = nb()
                mm_group(b, banks[b][:, 0:256], [(hT_own[:, kt, tk * 128:(tk + 1) * 128], wvb[:, kt, :]) for kt in range(8)], ["wvb"])
                if tk % 2 == 0:
                    act(Vb[:, tk, :], banks[b][:, 0:256], AF.Copy, [], [BK(b), ("V", tk)])
                else:
                    cp(Vb[:, tk, :], banks[b][:, 0:256], [], [BK(b), ("V", tk)])
            for hpl in range(2):
                hp = hq * 2 + hpl
                for i, base in enumerate((16, 24, 40)):
                    ldc(wq[i].rearrange("p a b -> p (a b)"), win_t[base + hp], [("wq", i)])
                ldc(btb, bt[hp], ["btb"])
                for bi in range(4):
                    c0 = 256 + bi * 512
                    b = nb()
                    mm_group(b, banks[b][:, :], [(wq[0][:, kt, :], hT_own[:, kt, c0:c0 + 512]) for kt in range(8)], [("wq", 0)])
                    act(qz[0][0:64, bi * 512:(bi + 1) * 512], banks[b][0:64, :], AF.Copy, [], [BK(b), ("qz", 0, bi)], scale=0.125)
                    ts(qz[1][64:128, bi * 512:(bi + 1) * 512], banks[b][64:128, :], 0.125, None, ALU.mult, ALU.bypass, [], [BK(b), ("qz", 1, bi)])
                for bi, (c0, w_) in enumerate(kvblocks):
                    b = nb()
                    mm_group(b, banks[b][:, 0:w_], [(wq[1][:, kt, :], hT_own[:, kt, c0:c0 + w_]) for kt in range(8)], [("wq", 1)])
                    act(kT[:, c0:c0 + w_], banks[b][:, 0:w_], AF.Copy, [], [BK(b), ("kT", bi)])
                att_flush()
                for bi in range(4):
                    c0 = 256 + bi * 512
                    b = nb()
                    mm_group(b, banks[b][:, :], [(wq[2][:, kt, :], hT_own[:, kt, c0:c0 + 512]) for kt in range(8)], [("wq", 2)])
                    tzb = tzbs[bi % 2]
                    act(tzb, banks[b][:, :], AF.Tanh, [], [BK(b), ("tzb", bi % 2)], scale=0.5)
                    stt(szb[:, bi * 512:(bi + 1) * 512], tzb, 1.0, banks[b][:, :], ALU.add, ALU.mult, [("tzb", bi % 2)], [BK(b), ("szb", bi)])
                last_hp = (hq == 3 and hpl == 1)
                for pi in range(16 + (2 if last_hp else 0)):
                    cur = att_S(hp, hpl, pi) if pi < 16 else None
                    old_ = pend.pop(0) if len(pend) == 2 or (cur is None and pend) else None
                    if old_ is not None:
                        att_ND(old_)
                        att_R(old_)
                    if cur is not None:
                        att_L(cur)
                        att_X(cur)
                        pend.append(cur)
                    if old_ is not None:
                        att_E(old_)
        P.barrier()

        t = R_TMP
        mT = r3(V16(t, 8 * OWN), 8); t += 8 * OWN * 2
        w4 = [r3(V16(t + i * 2048, 1024), 8) for i in range(8)]; t += 16384
        tg = [V32(t + i * 2048, 512) for i in range(4)]; t += 8192
        wost = V32(t, D); t += 4096
        wob = r3(V16(t, 8 * D), 8); t += 16384
        assert t - R_TMP <= TMP_BYTES, t - R_TMP
        for kt in range(8):
            ld(wost, wo_t[:, kt * D:(kt + 1) * D], ["wost"])
            stt(wob[:, kt, :], wost, 0.25, G_bc, ALU.mult, ALU.mult, ["wost"], [("wob", kt)])
        for ot in range(8):
            wb = (ot % 2) * 4
            srcs = (wpa_t[ot], wpb_t[ot], win_t[48 + ot], win_t[56 + ot])
            for i in range(4):
                ldc(w4[wb + i].rearrange("p a b -> p (a b)"), srcs[i], [("w4", wb + i)])
            for bi in range(4):
                c0 = 256 + bi * 512
                cs = slice(bi * 512, (bi + 1) * 512)
                bga = nb()
                mm_group(bga, banks[bga][:, :], [(w4[wb + 2][:, kt, :], hT_own[:, kt, c0:c0 + 512]) for kt in range(8)], [("w4", wb + 2)])
                act(tg[0], banks[bga][:, :], AF.Tanh, [], [BK(bga), ("tg", 0)], scale=0.5)
                bgb = nb()
                mm_group(bgb, banks[bgb][:, :], [(w4[wb + 3][:, kt, :], hT_own[:, kt, c0:c0 + 512]) for kt in range(8)], [("w4", wb + 3)])
                act(tg[1], banks[bgb][:, :], AF.Tanh, [], [BK(bgb), ("tg", 1)], scale=0.5)
                bpa = nb()
                mm_group(bpa, banks[bpa][:, :], [(w4[wb + 0][:, kt, :], yaT[:, kt, cs]) for kt in range(8)], [("w4", wb + 0)])
                stt(tg[2], tg[0], 1.0, banks[bpa][:, :], ALU.add, ALU.mult, [("tg", 0)], [BK(bpa), ("tg", 2)])
                bpb = nb()
                mm_group(bpb, banks[bpb][:, :], [(w4[wb + 1][:, kt, :], ybT[:, kt, cs]) for kt in range(8)], [("w4", wb + 1)])
                stt(tg[3], tg[1], 1.0, banks[bpb][:, :], ALU.add, ALU.mult, [("tg", 1)], [BK(bpb), ("tg", 3)])
                tt(mT[:, ot, cs], tg[2], tg[3], ALU.add, [("tg", 2), ("tg", 3)], [("mT", ot, bi)])
        P.barrier()

        t = 0
        NX2 = 4
        xt2 = [V32(t + i * 4096, D) for i in range(NX2)]; t += NX2 * 4096
        xn = [V32(t + i * 4096, D) for i in range(2)]; t += 8192
        ot_ = [V32(t + i * 4096, D) for i in range(3)]; t += 12288
        junk2 = V16(t, D); t += 2048
        assert t <= R_CONST
        def fin4(tk):
            bsel = tk % 2
            o3 = tk % 3
            stt(ot_[o3], xn[bsel], rstd[:, tk:tk + 1], gfin, ALU.mult, ALU.mult, [("xn", bsel, 0), ("xn", bsel, 1), ("rstd2", tk)], [("ot", o3)])
            P.dma("pool", (lambda tk=tk, o3=o3: lambda e: e.dma_start(out=out[tk * 128:(tk + 1) * 128, :], in_=ot_[o3]))(),
                  reads=[("ot", o3)], writes=[("outd", tk)])

        for tk in range(16):
            bsel = tk % 2
            x4 = tk % NX2
            ld(xt2[x4], xloc[OWN + tk * 128:OWN + (tk + 1) * 128, :], [("xt2", x4)])
            for hf in range(2):
                b = nb()
                mm_group(b, banks[b][:, :], [(mT[:, kt, tk * 128:(tk + 1) * 128], wob[:, kt, hf * 512:(hf + 1) * 512]) for kt in range(8)],
                         [("wob", kt) for kt in range(8)])
                tt(xn[bsel][:, hf * 512:(hf + 1) * 512], banks[b][:, :], xt2[x4][:, hf * 512:(hf + 1) * 512], ALU.add,
                   [("xt2", x4)], [BK(b), ("xn", bsel, hf)])
            act(junk2, xn[bsel], AF.Square, [("xn", bsel, 0), ("xn", bsel, 1)], ["junk2", ("ss2", tk)], accum=ss[:, tk:tk + 1])
            act(lnv[:, tk:tk + 1], ss[:, tk:tk + 1], AF.Ln, [("ss2", tk)], [("lnv2", tk)], bias=cols[:, 1:2], scale=1.0 / D)
            act(rstd[:, tk:tk + 1], lnv[:, tk:tk + 1], AF.Exp, [("lnv2", tk)], [("rstd2", tk)], scale=-0.5)
            if tk >= 1:
                fin4(tk - 1)
        fin4(15)
        P.final_wait("sp")
        P.emit(nc, st)
    return nc


def _tile_cols(w, ncol):
    K, C = w.shape
    return np.ascontiguousarray(w.reshape(8, 128, C // ncol, ncol).transpose(2, 1, 0, 3).reshape(C // ncol, 128, 8 * ncol))


def _col8(v):
    return np.ascontiguousarray(v.reshape(8, 128).T)


def _bias_tables(rpb, half):
    H = 16
    kp, kc = np.meshgrid(np.arange(2), np.arange(64), indexing="ij")
    kp = kp.reshape(128, 1); kc = kc.reshape(128, 1)
    j, s, qc = np.meshgrid(np.arange(5), np.arange(2), np.arange(64), indexing="ij")
    j = j.reshape(1, 640); s = s.reshape(1, 640); qc = qc.reshape(1, 640)
    out = np.full((8, 128, 3, 2, 640), NEG, np.float32)
    for var, pi in ((0, 0), (1, 14), (2, 15)):
        r_loc = 32 + 2 * pi
        R0 = min(max(r_loc - 4, 0), 54)
        kr_l = R0 + 2 * j + kp
        qr_l = r_loc + s + 0 * kp
        kc_l = kc + 0 * j
        qc_l = qc + 0 * kp
        if half == 1:
            kr, qr, kcg, qcg = kr_l, qr_l, kc_l, qc_l
        else:
            kr, qr, kcg, qcg = 63 - kr_l, 63 - qr_l, 63 - kc_l, 63 - qc_l
        r0 = np.clip(qr - 4, 0, 56)
        c0 = np.clip(qcg - 8, 0, 48)
        valid = (kr >= r0) & (kr <= r0 + 7) & (kcg >= c0) & (kcg <= c0 + 15) & (kr >= 0) & (kr <= 63)
        dr = np.clip(kr - qr + 7, 0, 14)
        dc = np.clip(kcg - qcg + 15, 0, 30)
        for h in range(H):
            vals = rpb[h][dr, dc]
            out[h // 2, :, var, h % 2, :] = np.where(valid, vals, np.float32(NEG))
    out = out.reshape(8, 128, 3, 2, 5, 128).transpose(0, 5, 2, 3, 4, 1)
    return np.ascontiguousarray(out.reshape(8, 128, 3 * 2 * 640))


_NC_CACHE = {}


def _prep_core(inp, b, half):
    f = np.float32
    xb = inp["x"][b]
    if half == 1:
        xloc = xb
    else:
        xloc = xb[::-1]
    m = {}
    m["xloc"] = np.ascontiguousarray(xloc, dtype=f)
    m["ccol"] = _col8(inp["c"][b])
    m["wc_t"] = _tile_cols(inp["w_c"][0], 256)
    m["bc_rep"] = np.ascontiguousarray(np.broadcast_to(inp["b_c"][0][None, :], (128, 3072)))
    m["gpre_rep"] = np.ascontiguousarray(np.broadcast_to(inp["g_pre"][0][None, :], (128, D)))
    m["gfin_rep"] = np.ascontiguousarray(np.broadcast_to(inp["g_final"][None, :], (128, D)))
    m["win_t"] = _tile_cols(inp["w_in"][0], 128)
    m["wv_t"] = _tile_cols(inp["w_in"][0][:, 4096:5120], 256)
    m["wpa_t"] = _tile_cols(inp["w_pa"][0], 128)
    m["wpb_t"] = _tile_cols(inp["w_pb"][0], 128)
    m["wo_t"] = np.ascontiguousarray(inp["w_o"][0].reshape(8, 128, D).transpose(1, 0, 2).reshape(128, 8 * D))
    cw = inp["conv_w"][0]
    wloc = np.zeros((5, D), f)
    if half == 1:
        wloc[0:4] = cw
    else:
        wloc[1:5] = cw[::-1]
    dcv = np.zeros((128, 8, 5, 128), f)
    idx = np.arange(128)
    for ct in range(8):
        for o_ in range(5):
            dcv[idx, ct, o_, idx] = wloc[o_, ct * 128 + idx]
    m["dconv"] = dcv.reshape(128, 8 * 5 * 128)
    if half == 1:
        gates = (inp["w_r_f"][0], inp["w_i_f"][0], inp["w_r_b"][0], inp["w_i_b"][0])
        vecs = (inp["conv_b"][0], inp["b_r_f"][0], inp["b_i_f"][0], inp["b_r_b"][0], inp["b_i_b"][0], inp["lam_f"][0], inp["lam_b"][0])
    else:
        gates = (inp["w_r_b"][0], inp["w_i_b"][0], inp["w_r_f"][0], inp["w_i_f"][0])
        vecs = (inp["conv_b"][0], inp["b_r_b"][0], inp["b_i_b"][0], inp["b_r_f"][0], inp["b_i_f"][0], inp["lam_b"][0], inp["lam_f"][0])
    gm = np.zeros((128, 8, 4, 128), f)
    for ct in range(8):
        for g in range(4):
            for hb_ in range(2):
                gm[hb_ * 64:(hb_ + 1) * 64, ct, g, hb_ * 64:(hb_ + 1) * 64] = gates[g][2 * ct + hb_]
    m["gmat"] = gm.reshape(128, 8 * 4 * 128)
    m["chvec"] = np.ascontiguousarray(np.stack([_col8(v) for v in vecs], axis=1).reshape(128, 56))
    m["bt"] = _bias_tables(inp["rpb"][0], half)
    m["ident"] = np.eye(128, dtype=f)
    return m


def kernel(**inputs):
    inp = {k: np.asarray(v, dtype=np.float32) for k, v in inputs.items()}
    if "nc" not in _NC_CACHE:
        _NC_CACHE["nc"] = build_nc()
    nc = _NC_CACHE["nc"]
    in_maps = []
    for core in range(8):
        in_maps.append(_prep_core(inp, core // 2, core % 2))
    res = run_bass_kernel_spmd(nc, in_maps, core_ids=list(range(8)))
    outp = np.zeros((4, NT, D), np.float32)
    for core in range(8):
        b, half = core // 2, core % 2
        o = np.asarray(res.results[core]["out"], dtype=np.float32)
        if half == 1:
            outp[b, OWN:] = o
        else:
            outp[b, 0:OWN] = o[::-1]
    return outp
```

```python
import numpy as np
from contextlib import ExitStack
import concourse.bass as bass
import concourse.mybir as mybir
from concourse.bass_utils import run_bass_kernel_spmd

F32 = mybir.dt.float32
BF16 = mybir.dt.bfloat16
AF = mybir.ActivationFunctionType
ALU = mybir.AluOpType

COMPUTE = ("pe", "act", "dve", "pool")
QUEUES = ("sp",) + COMPUTE


class Planner:
    def __init__(self, n_dma_sems=24):
        self.ops = {e: [] for e in QUEUES}
        self.count = {e: 0 for e in COMPUTE}
        self.waited = {e: {} for e in QUEUES}
        self.state = {}
        self.n_dma = n_dma_sems
        self.dma_i = 0
        self.dma_q = [0, 0]
        self.dma_cnt = [0] * n_dma_sems
        self.dma_last = [None] * n_dma_sems
        self.all_tickets = {}
        self.pe_mode = None
        self.pe_unsig = False

    def _deps(self, eng, reads, writes, extra=()):
        deps = {}

        def add(t):
            if t is None:
                return
            k, v = t
            if deps.get(k, 0) < v:
                deps[k] = v

        for k in reads:
            st = self.state.get(k)
            if st:
                add(st[0])
        for k in writes:
            st = self.state.get(k)
            if st:
                add(st[0])
                for t in st[1]:
                    add(t)
        for t in extra:
            add(t)
        waits = []
        for k, v in deps.items():
            if eng == "pe" and k == "pe":
                continue
            if self.waited[eng].get(k, 0) >= v:
                continue
            self.waited[eng][k] = v
            waits.append((k, v))
        return waits

    def _commit(self, ticket, reads, writes):
        for k in reads:
            st = self.state.setdefault(k, [None, []])
            st[1].append(ticket)
        for k in writes:
            self.state[k] = [ticket, []]
        self.all_tickets[ticket[0]] = max(self.all_tickets.get(ticket[0], 0), ticket[1])

    def op(self, eng, fn, reads=(), writes=(), signal=True, mode="mm"):
        waits = self._deps(eng, reads, writes)
        if eng == "pe":
            if self.pe_mode not in (None, mode):
                assert not self.pe_unsig
                v = self.count["pe"]
                if self.waited["pe"].get("pe", 0) < v:
                    self.waited["pe"]["pe"] = v
                    waits.append(("pe", v))
            self.pe_mode = mode
            self.pe_unsig = not signal
        if signal:
            self.count[eng] += 1
            ticket = (eng, self.count[eng])
            inc = (eng, 1)
        else:
            ticket = (eng, self.count[eng] + 1)
            inc = None
        self._commit(ticket, reads, writes)
        self.ops[eng].append((waits, fn, inc))
        return ticket

    def dma(self, queue, fn, reads=(), writes=()):
        half = self.n_dma // 2
        qi = 1 if queue == "pool" else 0
        i = qi * half + self.dma_q[qi] % half
        self.dma_q[qi] += 1
        key = ("dma", i)
        extra = [self.dma_last[i]] if self.dma_last[i] else []
        waits = self._deps(queue, reads, writes, extra)
        self.dma_cnt[i] += 16
        ticket = (key, self.dma_cnt[i])
        self.dma_last[i] = ticket
        self._commit(ticket, reads, writes)
        self.ops[queue].append((waits, fn, (key, 16)))
        return ticket

    def barrier(self):
        assert not self.pe_unsig
        for eng in QUEUES:
            waits = []
            for k, v in self.all_tickets.items():
                if self.waited[eng].get(k, 0) >= v:
                    continue
                self.waited[eng][k] = v
                waits.append((k, v))
            if waits:
                self.ops[eng].append((waits, None, None))
        self.state = {}

    def final_wait(self, eng="sp"):
        waits = []
        for k, v in self.all_tickets.items():
            if self.waited[eng].get(k, 0) >= v:
                continue
            self.waited[eng][k] = v
            waits.append((k, v))
        self.ops[eng].append((waits, None, None))

    def emit(self, nc, stack):
        sems = {}
        for e in COMPUTE:
            sems[e] = stack.enter_context(nc.semaphore("s_" + e))
        for i in range(self.n_dma):
            sems[("dma", i)] = stack.enter_context(nc.semaphore("s_dma%d" % i))
        block = stack.enter_context(nc.Block())

        def run(engname):
            def body(eng):
                for waits, fn, inc in self.ops[engname]:
                    for k, v in waits:
                        eng.wait_ge(sems[k], v)
                    if fn is None:
                        continue
                    ins = fn(eng)
                    if inc is not None:
                        ins.then_inc(sems[inc[0]], inc[1])
            return body

        block.sync(run("sp"))
        block.tensor(run("pe"))
        block.scalar(run("act"))
        block.vector(run("dve"))
        block.gpsimd(run("pool"))


D = 1024
NT = 4096
OWN = 2048
KV0 = 1792
NKV = NT - KV0
PS0 = 1920
NEG = -30000.0
EPS = 1e-6
ARENA_BYTES = 204 * 1024


def build_nc():
    nc = bass.Bass("TRN2", target_bir_lowering=False)

    def din(name, shape):
        return nc.dram_tensor(name, list(shape), F32, kind="ExternalInput").ap()

    xloc = din("xloc", [NT, D])
    ccol = din("ccol", [128, 8])
    wc_t = din("wc_t", [12, 128, 8 * 256])
    bc_rep = din("bc_rep", [128, 3072])
    gpre_rep = din("gpre_rep", [128, D])
    gfin_rep = din("gfin_rep", [128, D])
    win_t = din("win_t", [64, 128, 8 * 128])
    wv_t = din("wv_t", [4, 128, 8 * 256])
    wpa_t = din("wpa_t", [8, 128, 8 * 128])
    wpb_t = din("wpb_t", [8, 128, 8 * 128])
    wo_t = din("wo_t", [128, 8 * D])
    dconv = din("dconv", [128, 8 * 5 * 128])
    gmat = din("gmat", [128, 8 * 4 * 128])
    chvec = din("chvec", [128, 7 * 8])
    bt = din("bt", [8, 128, 3 * 2 * 640])
    ident = din("ident", [128, 128])
    out = nc.dram_tensor("out", [OWN, D], F32, kind="ExternalOutput").ap()

    P = Planner()
    with ExitStack() as st:
        arena = st.enter_context(nc.sbuf_tensor("arena", [128, ARENA_BYTES // 2], BF16))
        a16 = arena
        a32 = arena.bitcast(F32)
        psall = st.enter_context(nc.psum_tensor("psall", [128, 8 * 512], F32))
        psall16 = psall.bitcast(BF16)
        banks = [psall[:, i * 512:(i + 1) * 512] for i in range(8)]
        banks16 = [psall16[:, i * 1024:(i + 1) * 1024] for i in range(8)]

        def V16(off, n):
            assert off % 2 == 0 and off + 2 * n <= ARENA_BYTES, (off, n)
            return a16[:, off // 2: off // 2 + n]

        def V32(off, n):
            assert off % 4 == 0 and off + 4 * n <= ARENA_BYTES, (off, n)
            return a32[:, off // 4: off // 4 + n]

        def r3(ap, a):
            return ap.rearrange("p (a b) -> p a b", a=a)

        R_HOWN = 0
        R_YA = R_HOWN + 8 * NKV * 2
        R_YB = R_YA + 8 * OWN * 2
        R_CONST = R_YB + 8 * OWN * 2
        o = R_CONST
        ident_b = V16(o, 128); o += 256
        ones_b = V16(o, 128); o += 256
        chv = r3(V32(o, 56), 7); o += 224
        der = r3(V32(o, 80), 10); o += 320
        cols = V32(o, 8); o += 32
        carry = V32(o, 8); o += 32
        ss = V32(o, 32); o += 128
        lnv = V32(o, 32); o += 128
        rstd = V32(o, 32); o += 128
        dconv_b = V16(o, 5120); o += 10240
        gmat_b = V16(o, 4096); o += 8192
        G_bc = V32(o, D); o += 4096
        gfin = V32(o, D); o += 4096
        R_TMP = (o + 63) // 64 * 64
        TMP_BYTES = ARENA_BYTES - R_TMP

        hT_own = r3(V16(R_HOWN, 8 * NKV), 8)
        yaT = r3(V16(R_YA, 8 * OWN), 8)
        ybT = r3(V16(R_YB, 8 * OWN), 8)
        hT_oth = r3(V16(R_YB, 8 * OWN), 8)

        bank_rr = [0]

        def nb():
            b = bank_rr[0] % 8
            bank_rr[0] += 1
            return b

        def nb2():
            if bank_rr[0] % 2:
                bank_rr[0] += 1
            b = bank_rr[0] % 8
            bank_rr[0] += 2
            return b

        def BK(b):
            return ("ps", b)

        def mm_group(bank, out_ap, pairs, reads, mode="mm"):
            n = len(pairs)
            for i, (l, r) in enumerate(pairs):
                P.op("pe", (lambda l=l, r=r, s=(i == 0), e_=(i == n - 1):
                            lambda e: e.matmul(out_ap, l, r, start=s, stop=e_))(),
                     reads=reads, writes=[BK(bank)], signal=(i == n - 1), mode=mode)

        def act(out_ap, in_ap, func, reads, writes, bias=None, scale=None, accum=None):
            kw = {}
            if bias is not None:
                kw["bias"] = bias
            if scale is not None:
                kw["scale"] = scale
            if accum is not None:
                kw["accum_out"] = accum
            P.op("act", lambda e: e.activation(out_ap, in_ap, func, **kw), reads=reads, writes=writes)

        def stt(out_ap, in0, scalar, in1, op0, op1, reads, writes, eng="dve"):
            P.op(eng, lambda e: e.scalar_tensor_tensor(out_ap, in0, scalar, in1, op0, op1), reads=reads, writes=writes)

        def tt(out_ap, in0, in1, op, reads, writes, eng="dve"):
            P.op(eng, lambda e: e.tensor_tensor(out_ap, in0, in1, op), reads=reads, writes=writes)

        def ts(out_ap, in0, s1, s2, op0, op1, reads, writes, eng="dve"):
            P.op(eng, lambda e: e.tensor_scalar(out_ap, in0, s1, s2, op0, op1), reads=reads, writes=writes)

        def cp(out_ap, in_ap, reads, writes, eng="dve"):
            P.op(eng, lambda e: e.tensor_copy(out_ap, in_ap), reads=reads, writes=writes)

        def ms(ap, val, writes, eng="dve"):
            P.op(eng, lambda e: e.memset(ap, val), writes=writes)

        def ld(out_ap, in_ap, writes, q="sp", reads=()):
            P.dma(q, lambda e: e.dma_start(out=out_ap, in_=in_ap), reads=reads, writes=writes)

        def ldc(out_ap, in_ap, writes, reads=()):
            P.dma("pool", lambda e: e.dma_start(out=out_ap, in_=in_ap, max_dma_last_dim=4096), reads=reads, writes=writes)

        t = R_TMP
        NWS = 3
        wcs = [V32(t + i * 8192, 2048) for i in range(NWS)]; t += NWS * 8192
        wcb = [r3(V16(t + i * 4096, 2048), 8) for i in range(2)]; t += 8192
        sc_rep = r3(V16(t, 1024), 8); t += 2048
        bcr = V32(t, 3072); t += 12288
        gpr = V32(t, D); t += 4096
        cc = V32(t, 8); t += 32
        tz0 = V32(t, 8); t += 32
        s2 = V32(t, 8); t += 32
        ev = V32(t, 16); t += 64
        assert t - R_TMP <= TMP_BYTES
        mod_bc = V32(R_YA, 3072)
        A_bc = V32(R_YA + 12288, D)
        B_bc = mod_bc[:, 0:D]

        ld(cc, ccol, ["cc"])
        ld(chv.rearrange("p a b -> p (a b)"), chvec, ["chv"])
        for nbk in range(NWS):
            ld(wcs[nbk], wc_t[nbk], [("wcs", nbk)], q="sp")
        ld(bcr, bc_rep, ["bcr"])
        ld(gpr, gpre_rep, ["gpr"])
        ldc(ident_b, ident, ["ident"])
        ms(ones_b, 1.0, ["ones"])
        ms(cols[:, 0:1], 1.0, ["cols"])
        ms(cols[:, 1:2], EPS, ["cols"])
        ms(carry, 0.0, ["carry"])
        dcv = r3(dconv_b, 40)
        gmv = r3(gmat_b, 32)

        act(tz0, cc, AF.Tanh, ["cc"], ["tz0"], scale=0.5)
        stt(s2, tz0, 1.0, cc, ALU.add, ALU.mult, ["tz0", "cc"], ["s2"])
        for kt in range(8):
            ts(sc_rep[:, kt, :], ones_b, s2[:, kt:kt + 1], 0.5, ALU.mult, ALU.mult, ["ones", "s2"], [("scr", kt)])
        for nbk in range(12):
            w = wcb[nbk % 2]
            ws_ = nbk % NWS
            act(w.rearrange("p a b -> p (a b)"), wcs[ws_], AF.Copy, [("wcs", ws_)], [("wcb", nbk % 2)])
            if nbk + NWS < 12:
                ld(wcs[ws_], wc_t[nbk + NWS], [("wcs", ws_)], q="sp")
            b = nb()
            mm_group(b, banks[b][:, 0:256], [(sc_rep[:, kt, :], w[:, kt, :]) for kt in range(8)],
                     [("wcb", nbk % 2)] + [("scr", kt) for kt in range(8)])
            tt(mod_bc[:, nbk * 256:(nbk + 1) * 256], banks[b][:, 0:256], bcr[:, nbk * 256:(nbk + 1) * 256], ALU.add,
               ["bcr"], [BK(b), ("mod", nbk)])
        stt(A_bc, mod_bc[:, D:2 * D], 1.0, gpr, ALU.add, ALU.mult, [("mod", k) for k in range(4, 8)] + ["gpr"], ["A_bc"])
        cp(G_bc, mod_bc[:, 2 * D:3 * D], [("mod", k) for k in range(8, 12)], ["G_bc"])
        ld(gfin, gfin_rep, ["gfin"])
        for g in range(4):
            ts(der[:, g, :], chv[:, 1 + g, :], 0.5, None, ALU.mult, ALU.bypass, ["chv"], [("der", g)])
        act(ev, chv[:, 5:7, :].rearrange("p a b -> p (a b)"), AF.Exp, ["chv"], ["ev"], scale=-1.0)
        act(ev, ev, AF.Ln, ["ev", "cols"], ["ev"], bias=cols[:, 0:1], scale=1.0)
        for q in range(2):
            ts(der[:, 4 + 2 * q, :], ev[:, q * 8:(q + 1) * 8], -8.0, None, ALU.mult, ALU.bypass, ["ev"], [("der", 4 + 2 * q)])
            ts(der[:, 5 + 2 * q, :], ev[:, q * 8:(q + 1) * 8], -4.0, None, ALU.mult, ALU.bypass, ["ev"], [("der", 5 + 2 * q)])

        NXT = 4
        xt = [V32(t + i * 4096, D) for i in range(NXT)]; t += NXT * 4096
        junk = V16(t, D); t += 2048
        t1 = V32(t, D); t += 4096
        hb = [V16(t + i * 2048, D) for i in range(2)]; t += 4096
        assert t - R_TMP <= TMP_BYTES
        p1bank = {}

        def p1A(tk):
            x3 = tk % NXT
            ld(xt[x3], xloc[tk * 128:(tk + 1) * 128, :], [("xt", x3)])
            act(junk, xt[x3], AF.Square, [("xt", x3)], ["junk", ("ss", tk)], accum=ss[:, tk:tk + 1])
            act(lnv[:, tk:tk + 1], ss[:, tk:tk + 1], AF.Ln, [("ss", tk), "cols"], [("lnv", tk)], bias=cols[:, 1:2], scale=1.0 / D)
            act(rstd[:, tk:tk + 1], lnv[:, tk:tk + 1], AF.Exp, [("lnv", tk)], [("rstd", tk)], scale=-0.5)

        def p1B(tk):
            x3 = tk % NXT
            bsel = tk % 2
            stt(t1, xt[x3], rstd[:, tk:tk + 1], A_bc, ALU.mult, ALU.mult, [("xt", x3), ("rstd", tk), "A_bc"], ["t1"])
            tt(hb[bsel], t1, B_bc, ALU.add, ["t1"] + [("mod", k) for k in range(4)], [("hb", bsel)])
            b = nb()
            p1bank[tk] = b
            for kt in range(8):
                P.op("pe", (lambda kt=kt, b=b, bsel=bsel: lambda e: e.transpose(
                    banks16[b][:, kt * 128:(kt + 1) * 128], hb[bsel][:, kt * 128:(kt + 1) * 128], ident_b))(),
                    reads=[("hb", bsel), "ident"], writes=[BK(b)], signal=(kt == 7), mode="tr")

        def p1C(tk):
            b = p1bank[tk]
            src = r3(banks16[b][:, :], 8)
            if tk < 16:
                act(hT_oth[:, :, tk * 128:(tk + 1) * 128], src, AF.Copy, [], [BK(b), ("hoth", tk)])
            if tk >= 16:
                j = tk - 14
                act(hT_own[:, :, j * 128:(j + 1) * 128], src, AF.Copy, [], [BK(b), ("hown", j)])
            elif tk >= 14:
                j = tk - 14
                cp(hT_own[:, :, j * 128:(j + 1) * 128], src, [], [BK(b), ("hown", j)])

        for s_ in range(34):
            if s_ < 32:
                p1A(s_)
            if s_ == 10:
                ldc(dconv_b, dconv, ["dconv"], reads=[("xt", 10 % NXT)])
                ldc(gmat_b, gmat, ["gmat"], reads=[("xt", 10 % NXT)])
            if 0 <= s_ - 1 < 32:
                p1B(s_ - 1)
            if 0 <= s_ - 2 < 32:
                p1C(s_ - 2)
        P.barrier()

        def branchA(part):
            own = (part == 1)
            t = R_TMP
            WP = 2176 if own else 2048
            SEG = []
            for q_ in range(2 if own else 1):
                wseg = WP if q_ == 0 else OWN
                SEG.append(dict(TR=V32(t, wseg), TI=V32(t + wseg * 4, wseg), A2=V32(t + wseg * 8, wseg), U=V32(t + wseg * 12, wseg)))
                t += wseg * 16
            if not own:
                SEG0_alt = dict(TR=V32(t, WP), TI=V32(t + WP * 4, WP)); t += WP * 8
                wA2 = [r3(V16(t + i * 2048, 1024), 8) for i in range(2)]; t += 4096
                wA = [wA2[0], None, wA2[1], None]
            else:
                wA = [r3(V16(t + i * 2048, 1024), 8) for i in range(4)]; t += 8192
            if own:
                assert t - R_TMP <= TMP_BYTES, t - R_TMP
                t = R_YB
                lim = R_YB + 8 * OWN * 2
            else:
                lim = R_TMP + TMP_BYTES
            NX = NKV if own else 2048
            xa_bf = V16(t, NX + 4); t += (NX + 4) * 2
            t = (t + 3) // 4 * 4
            xc_f = V32(t, NX); t += NX * 4
            if not own:
                xc_f2 = V32(t, NX); t += NX * 4
            xc_b = V16(t, NX); t += NX * 2
            t = (t + 3) // 4 * 4
            if own:
                HP = V32(t, 2176); t += 2176 * 4
                NTZ = 5
                tzt = [V32(t + i * 1024, 256) for i in range(NTZ)]; t += NTZ * 1024
            assert t <= lim, (t, lim)
            hsrc = hT_own if own else hT_oth
            xblocks = [(i * 512, 512) for i in range(NX // 512)] + ([(2048, 256)] if own else [])
            nxb = len(xblocks)
            xblocks_i = [(i, c0, w_) for i, (c0, w_) in enumerate(xblocks)]
            ms(xa_bf[:, 0:2], 0.0, [("xa", "padl")])
            ms(xa_bf[:, NX + 2:NX + 4], 0.0, [("xa", "padr")])

            def loadw(ct):
                ldc(wA[(ct % 2) * 2].rearrange("p a b -> p (a b)"), win_t[ct], [("wA", (ct % 2) * 2)])
                if own:
                    ldc(wA[(ct % 2) * 2 + 1].rearrange("p a b -> p (a b)"), win_t[8 + ct], [("wA", (ct % 2) * 2 + 1)])

            def segbuf(q_, ct):
                S_ = SEG[q_]
                if (not own) and ct % 2 == 1:
                    S_ = dict(S_, TR=SEG0_alt["TR"], TI=SEG0_alt["TI"])
                return S_

            def sk(q_, gi, ct):
                if (not own) and gi in (0, 1):
                    return ("seg", q_, gi, ct % 2)
                return ("seg", q_, gi)

            loadw(0)
            segs = [(0, 0, (PS0 - KV0) if own else 0, 2176 if own else PS0)]
            if own:
                segs.append((1, 2, 256, OWN))
            def XCF(ct):
                return xc_f if (own or ct % 2 == 0) else xc_f2

            def xfkey(ct, bi):
                return ("xcf", bi) if own else ("xcf", ct % 2, bi)

            def stage_F(ct):
                wx = wA[(ct % 2) * 2]
                wz = wA[(ct % 2) * 2 + 1]
                if ct + 1 < 8:
                    loadw(ct + 1)
                pairs = [xblocks_i[i:i + 2] for i in range(0, nxb, 2)]
                for pr in pairs:
                    b = nb2()
                    for k_, (bi, c0, w_) in enumerate(pr):
                        mm_group(b + k_, banks[b + k_][:, 0:w_], [(wx[:, kt, :], hsrc[:, kt, c0:c0 + w_]) for kt in range(8)],
                                 [("wA", (ct % 2) * 2)])
                    c0 = pr[0][1]
                    W = sum(p_[2] for p_ in pr)
                    act(xa_bf[:, 2 + c0:2 + c0 + W], psall[:, b * 512:b * 512 + W], AF.Copy, [],
                        [BK(b + k_) for k_ in range(len(pr))] + [("xa", p_[0]) for p_ in pr])
                for pr in pairs:
                    b = nb2()
                    for k_, (bi, c0, w_) in enumerate(pr):
                        rk = [("xa", k) for k in range(max(0, bi - 1), min(nxb, bi + 2))] + [("xa", "padl"), ("xa", "padr"), "dconv"]
                        mm_group(b + k_, banks[b + k_][:, 0:w_], [(dcv[:, ct * 5 + o_, :], xa_bf[:, c0 + o_:c0 + o_ + w_]) for o_ in range(5)], rk)
                    c0 = pr[0][1]
                    W = sum(p_[2] for p_ in pr)
                    bks = [BK(b + k_) for k_ in range(len(pr))]
                    src_ = psall[:, b * 512:b * 512 + W]
                    if own:
                        act(xc_b[:, c0:c0 + W], src_, AF.Identity, ["chv"], bks + [("xcb", p_[0]) for p_ in pr], bias=chv[:, 0, ct:ct + 1])
                        act(XCF(ct)[:, c0:c0 + W], src_, AF.Identity, ["chv"], bks + [xfkey(ct, p_[0]) for p_ in pr], bias=chv[:, 0, ct:ct + 1])
                    else:
                        ts(XCF(ct)[:, c0:c0 + W], src_, chv[:, 0, ct:ct + 1], None, ALU.add, ALU.bypass, ["chv"], bks + [xfkey(ct, p_[0]) for p_ in pr])
                        cp(xc_b[:, c0:c0 + W], XCF(ct)[:, c0:c0 + W], [xfkey(ct, p_[0]) for p_ in pr], [("xcb", p_[0]) for p_ in pr])
                if own:
                    for bi in range(8):
                        c0 = 256 + bi * 256
                        b = nb()
                        tz = tzt[(ct * 8 + bi) % NTZ]
                        tk_ = ("tzt", (ct * 8 + bi) % NTZ)
                        mm_group(b, banks[b][:, 0:256], [(wz[:, kt, :], hT_own[:, kt, c0:c0 + 256]) for kt in range(8)],
                                 [("wA", (ct % 2) * 2 + 1)])
                        act(tz, banks[b][:, 0:256], AF.Tanh, [], [BK(b), tk_], scale=0.5)
                        stt(yaT[:, ct, bi * 256:(bi + 1) * 256], tz, 1.0, banks[b][:, 0:256], ALU.add, ALU.mult, [tk_], [BK(b), ("sza", ct, bi)])

            def stage_G1(ct):
                for (q_, g0, s0, wd) in segs:
                    S_ = segbuf(q_, ct)
                    c = 0
                    while c < wd:
                        w1 = min(512, wd - c)
                        w2 = min(512, wd - c - w1)
                        w_ = w1 + w2
                        xk = [("xcb", k) for k in range((s0 + c) // 512, min(nxb, (s0 + c + w_ - 1) // 512 + 1))]
                        for gi, dst in ((0, S_["TR"]), (1, S_["TI"])):
                            b = nb2()
                            mm_group(b, banks[b][:, 0:w1], [(gmv[:, ct * 4 + g0 + gi, :], xc_b[:, s0 + c:s0 + c + w1])], xk + ["gmat"])
                            bks = [BK(b)]
                            if w2 > 0:
                                mm_group(b + 1, banks[b + 1][:, 0:w2], [(gmv[:, ct * 4 + g0 + gi, :], xc_b[:, s0 + c + w1:s0 + c + w_])], xk + ["gmat"])
                                bks.append(BK(b + 1))
                            act(dst[:, c:c + w_], psall[:, b * 512:b * 512 + w_], AF.Tanh, [("der", g0 + gi)], bks + [sk(q_, gi, ct)],
                                bias=der[:, g0 + gi, ct:ct + 1], scale=0.5)
                        c += w_
                    hcP = der[:, 5 + g0, ct:ct + 1]
                    act(S_["TR"][:, 0:wd], S_["TR"][:, 0:wd], AF.Exp, [sk(q_, 0, ct), ("der", 5 + g0)], [sk(q_, 0, ct)], bias=hcP, scale=hcP)
                    tt(S_["A2"][:, 0:wd], S_["TR"][:, 0:wd], S_["TR"][:, 0:wd], ALU.mult, [sk(q_, 0, ct)], [("seg", q_, 2)], eng="pool")

            def stage_G2(ct):
                for (q_, g0, s0, wd) in segs:
                    S_ = segbuf(q_, ct)
                    act(S_["A2"][:, 0:wd], S_["A2"][:, 0:wd], AF.Sqrt, [("seg", q_, 2), "cols"], [("seg", q_, 2)], bias=cols[:, 0:1], scale=-1.0)

            def stage_D(ct):
                for (q_, g0, s0, wd) in segs:
                    S_ = segbuf(q_, ct)
                    TR, TI, A2, U = S_["TR"], S_["TI"], S_["A2"], S_["U"]
                    xfk = [xfkey(ct, k) for k in range(s0 // 512, min(nxb, (s0 + wd - 1) // 512 + 1))]
                    stt(U[:, 0:wd], TI[:, 0:wd], 1.0, XCF(ct)[:, s0:s0 + wd], ALU.add, ALU.mult, [sk(q_, 1, ct)] + xfk, [("seg", q_, 3)])
                    stt(U[:, 0:wd], U[:, 0:wd], 0.5, A2[:, 0:wd], ALU.mult, ALU.mult, [("seg", q_, 3), ("seg", q_, 2)], [("seg", q_, 3)])
                    if not own:
                        P.op("dve", (lambda TI=TI, TR=TR, U=U, wd=wd: lambda e: e.tensor_tensor_scan(TI[:, 0:wd], TR[:, 0:wd], U[:, 0:wd], 0.0, ALU.mult, ALU.add))(),
                             reads=[sk(q_, 0, ct), ("seg", q_, 3)], writes=[sk(q_, 1, ct)])
                        cp(carry[:, ct:ct + 1], TI[:, wd - 1:wd], [sk(q_, 1, ct)], [("carry", ct)])
                    elif q_ == 0:
                        P.op("dve", (lambda TR=TR, U=U, wd=wd, ct=ct: lambda e: e.tensor_tensor_scan(HP[:, 0:wd], TR[:, 0:wd], U[:, 0:wd], carry[:, ct:ct + 1], ALU.mult, ALU.add))(),
                             reads=[sk(q_, 0, ct), ("seg", q_, 3), ("carry", ct)], writes=["HP"])
                    else:
                        P.op("dve", (lambda A2=A2, TR=TR, U=U, wd=wd: lambda e: e.tensor_tensor_scan(A2[:, 0:wd][:, ::-1], TR[:, 0:wd][:, ::-1], U[:, 0:wd][:, ::-1], 0.0, ALU.mult, ALU.add))(),
                             reads=[("seg", q_, 0), ("seg", q_, 3)], writes=[("seg", q_, 2)])
                        tt(HP[:, 128:2176], HP[:, 128:2176], A2[:, 0:OWN], ALU.add, ["HP", ("seg", q_, 2)], ["HP"], eng="pool")
                        tt(yaT[:, ct, :], HP[:, 128:2176], yaT[:, ct, :], ALU.mult, ["HP"] + [("sza", ct, k) for k in range(8)], [("ya", ct)], eng="pool")

            stage_F(0)
            for ct in range(8):
                stage_G1(ct)
                if (not own) and ct + 1 < 8:
                    stage_F(ct + 1)
                stage_G2(ct)
                stage_D(ct)
                if own and ct + 1 < 8:
                    stage_F(ct + 1)
            P.barrier()

        branchA(0)
        branchA(1)

        t = R_TMP
        wvb = r3(V16(t, 2048), 8); t += 4096
        Vb = r3(V16(t, 18 * 256), 18); t += 9216
        wq = [r3(V16(t + i * 2048, 1024), 8) for i in range(3)]; t += 6144
        btb = V16(t, 3840); t += 7680
        qz = [V16(t + i * 4096, OWN) for i in range(2)]; t += 8192
        kT = V16(t, NKV); t += NKV * 2
        szb = V16(t, OWN); t += 4096
        Lg = [V32(t + i * 5120, 1280) for i in range(2)]; t += 10240
        PT = [V16(t + i * 2560, 1280) for i in range(3)]; t += 7680
        lnd = [V32(t + i * 512, 128) for i in range(2)]; t += 1024
        rden = [V32(t + i * 512, 128) for i in range(2)]; t += 1024
        tn = [V32(t + i * 512, 128) for i in range(2)]; t += 1024
        tzbs = [V32(t + i * 2048, 512) for i in range(2)]; t += 4096
        assert t - R_TMP <= TMP_BYTES, t - R_TMP
        ms(qz[0][64:128, :], 0.0, [("qzz", 0)])
        ms(qz[1][0:64, :], 0.0, [("qzz", 1)])
        kvblocks = [(i * 512, 512) for i in range(4)] + [(2048, 256)]
        SB = [[0, 1, 2], [3, 4, 5]]
        NDB = [6, 7]
        step = [0]

        def att_S(hp, hpl, pi):
            i = step[0]
            step[0] += 1
            sl = i % 2
            r_loc = 32 + 2 * pi
            R0 = min(max(r_loc - 4, 0), 54)
            koff = (R0 - 28) * 64
            qoff = pi * 128
            var = 0 if pi <= 13 else pi - 13
            kkeys = [("kT", k) for k in range(koff // 512, (koff + 639) // 512 + 1)]
            B = SB[sl]
            for e_ in range(2):
                for j in range(5):
                    if j < 4:
                        bb, oc = B[e_], j * 128
                    else:
                        bb, oc = B[2], e_ * 128
                    for kh in range(2):
                        P.op("pe", (lambda bb=bb, oc=oc, kh=kh, j=j, e_=e_: lambda e: e.matmul(
                            banks[bb][kh * 64:(kh + 1) * 64, oc:oc + 128],
                            kT[:, koff + 128 * j + 64 * kh:koff + 128 * j + 64 * (kh + 1)],
                            qz[e_][:, qoff:qoff + 128], start=True, stop=False))(),
                            reads=kkeys + [("qz", e_, pi // 4), ("qzz", e_)], writes=[BK(bb)], signal=False, mode="ct")
                        bcol = ((var * 2 + e_) * 5 + j) * 128 + kh * 64
                        P.op("pe", (lambda bb=bb, oc=oc, kh=kh, bcol=bcol: lambda e: e.matmul(
                            banks[bb][kh * 64:(kh + 1) * 64, oc:oc + 128],
                            btb[:, bcol:bcol + 64], ident_b, start=False, stop=True))(),
                            reads=["btb", "ident"], writes=[BK(bb)], signal=(kh == 1), mode="ct")
            return dict(hp=hp, hpl=hpl, pi=pi, sl=sl, tk0=koff // 128, qoff=qoff, p3=i % 3, var=var)

        def att_L(d):
            pass

        def att_X(d):
            sl, p3 = d["sl"], d["p3"]
            B = SB[sl]
            for e_ in range(2):
                act(PT[p3][:, e_ * 640:e_ * 640 + 512], banks[B[e_]][:, :], AF.Exp, [], [BK(B[e_]), ("PT", p3, e_)])
            act(r3(PT[p3], 2)[:, :, 512:640], r3(banks[B[2]][:, 0:256], 2), AF.Exp, [], [BK(B[2]), ("PT", p3, 2)])

        def att_ND(d):
            sl, p3, tk0, hpl = d["sl"], d["p3"], d["tk0"], d["hpl"]
            bn_ = NDB[sl]
            vk = [("V", tk0 + j) for j in range(5)]
            for which in range(2):
                for j in range(5):
                    for e_ in range(2):
                        hs = slice(e_ * 64, (e_ + 1) * 64)
                        rhs_ = PT[p3][:, e_ * 640 + j * 128:e_ * 640 + (j + 1) * 128]
                        if which == 0:
                            lhs_ = Vb[:, tk0 + j, hpl * 128 + e_ * 64:hpl * 128 + (e_ + 1) * 64]
                            out_ = banks[bn_][hs, 0:128]
                        else:
                            lhs_ = ones_b[:, 0:64]
                            out_ = banks[bn_][hs, 128:256]
                        P.op("pe", (lambda out_=out_, lhs_=lhs_, rhs_=rhs_, j=j: lambda e: e.matmul(
                            out_, lhs_, rhs_, start=(j == 0), stop=(j == 4)))(),
                            reads=[("PT", p3, 0), ("PT", p3, 1), ("PT", p3, 2), "ones"] + vk, writes=[BK(bn_)], signal=(j == 4 and e_ == 1), mode="ct")

        def att_R(d):
            sl = d["sl"]
            bn_ = NDB[sl]
            P.op("dve", (lambda sl=sl, bn_=bn_: lambda e: e.reciprocal(rden[sl], banks[bn_][:, 128:256]))(),
                 reads=[], writes=[BK(bn_), ("rden", sl)])

        def att_E(d):
            sl, hp, pi, qoff = d["sl"], d["hp"], d["pi"], d["qoff"]
            bn_ = NDB[sl]
            tt(tn[sl], banks[bn_][:, 0:128], rden[sl], ALU.mult, [("rden", sl)], [BK(bn_), ("tn", sl)])
            tt(ybT[:, hp, qoff:qoff + 128], tn[sl], szb[:, qoff:qoff + 128], ALU.mult,
               [("tn", sl), ("szb", pi // 4)], [("yb", hp, pi)])

        pend = []

        def att_flush():
            while pend:
                old_ = pend.pop(0)
                att_ND(old_)
                att_R(old_)
                att_E(old_)

        for hq in range(4):
            att_flush()
            ldc(wvb.rearrange("p a b -> p (a b)"), wv_t[hq], ["wvb"])
            for tk in range(18):
                b = nb()
                mm_group(b, banks[b][:, 0:256], [(hT_own[:, kt, tk * 128:(tk + 1) * 128], wvb[:, kt, :]) for kt in range(8)], ["wvb"])
                if tk % 2 == 0:
                    act(Vb[:, tk, :], banks[b][:, 0:256], AF.Copy, [], [BK(b), ("V", tk)])
                else:
                    cp(Vb[:, tk, :], banks[b][:, 0:256], [], [BK(b), ("V", tk)])
            for hpl in range(2):
                hp = hq * 2 + hpl
                for i, base in enumerate((16, 24, 40)):
                    ldc(wq[i].rearrange("p a b -> p (a b)"), win_t[base + hp], [("wq", i)])
                ldc(btb, bt[hp], ["btb"])
                for bi in range(4):
                    c0 = 256 + bi * 512
                    b = nb()
                    mm_group(b, banks[b][:, :], [(wq[0][:, kt, :], hT_own[:, kt, c0:c0 + 512]) for kt in range(8)], [("wq", 0)])
                    act(qz[0][0:64, bi * 512:(bi + 1) * 512], banks[b][0:64, :], AF.Copy, [], [BK(b), ("qz", 0, bi)], scale=0.125)
                    ts(qz[1][64:128, bi * 512:(bi + 1) * 512], banks[b][64:128, :], 0.125, None, ALU.mult, ALU.bypass, [], [BK(b), ("qz", 1, bi)])
                for bi, (c0, w_) in enumerate(kvblocks):
                    b = nb()
                    mm_group(b, banks[b][:, 0:w_], [(wq[1][:, kt, :], hT_own[:, kt, c0:c0 + w_]) for kt in range(8)], [("wq", 1)])
                    act(kT[:, c0:c0 + w_], banks[b][:, 0:w_], AF.Copy, [], [BK(b), ("kT", bi)])
                att_flush()
                for bi in range(4):
                    c0 = 256 + bi * 512
                    b = nb()
                    mm_group(b, banks[b][:, :], [(wq[2][:, kt, :], hT_own[:, kt, c0:c0 + 512]) for kt in range(8)], [("wq", 2)])
                    tzb = tzbs[bi % 2]
                    act(tzb, banks[b][:, :], AF.Tanh, [], [BK(b), ("tzb", bi % 2)], scale=0.5)
                    stt(szb[:, bi * 512:(bi + 1) * 512], tzb, 1.0, banks[b][:, :], ALU.add, ALU.mult, [("tzb", bi % 2)], [BK(b), ("szb", bi)])
                last_hp = (hq == 3 and hpl == 1)
                for pi in range(16 + (2 if last_hp else 0)):
                    cur = att_S(hp, hpl, pi) if pi < 16 else None
                    old_ = pend.pop(0) if len(pend) == 2 or (cur is None and pend) else None
                    if old_ is not None:
                        att_ND(old_)
                        att_R(old_)
                    if cur is not None:
                        att_L(cur)
                        att_X(cur)
                        pend.append(cur)
                    if old_ is not None:
                        att_E(old_)
        P.barrier()

        t = R_TMP
        mT = r3(V16(t, 8 * OWN), 8); t += 8 * OWN * 2
        w4 = [r3(V16(t + i * 2048, 1024), 8) for i in range(8)]; t += 16384
        tg = [V32(t + i * 2048, 512) for i in range(4)]; t += 8192
        wost = V32(t, D); t += 4096
        wob = r3(V16(t, 8 * D), 8); t += 16384
        assert t - R_TMP <= TMP_BYTES, t - R_TMP
        for kt in range(8):
            ld(wost, wo_t[:, kt * D:(kt + 1) * D], ["wost"])
            stt(wob[:, kt, :], wost, 0.25, G_bc, ALU.mult, ALU.mult, ["wost"], [("wob", kt)])
        for ot in range(8):
            wb = (ot % 2) * 4
            srcs = (wpa_t[ot], wpb_t[ot], win_t[48 + ot], win_t[56 + ot])
            for i in range(4):
                ldc(w4[wb + i].rearrange("p a b -> p (a b)"), srcs[i], [("w4", wb + i)])
            for bi in range(4):
                c0 = 256 + bi * 512
                cs = slice(bi * 512, (bi + 1) * 512)
                bga = nb()
                mm_group(bga, banks[bga][:, :], [(w4[wb + 2][:, kt, :], hT_own[:, kt, c0:c0 + 512]) for kt in range(8)], [("w4", wb + 2)])
                act(tg[0], banks[bga][:, :], AF.Tanh, [], [BK(bga), ("tg", 0)], scale=0.5)
                bgb = nb()
                mm_group(bgb, banks[bgb][:, :], [(w4[wb + 3][:, kt, :], hT_own[:, kt, c0:c0 + 512]) for kt in range(8)], [("w4", wb + 3)])
                act(tg[1], banks[bgb][:, :], AF.Tanh, [], [BK(bgb), ("tg", 1)], scale=0.5)
                bpa = nb()
                mm_group(bpa, banks[bpa][:, :], [(w4[wb + 0][:, kt, :], yaT[:, kt, cs]) for kt in range(8)], [("w4", wb + 0)])
                stt(tg[2], tg[0], 1.0, banks[bpa][:, :], ALU.add, ALU.mult, [("tg", 0)], [BK(bpa), ("tg", 2)])
                bpb = nb()
                mm_group(bpb, banks[bpb][:, :], [(w4[wb + 1][:, kt, :], ybT[:, kt, cs]) for kt in range(8)], [("w4", wb + 1)])
                stt(tg[3], tg[1], 1.0, banks[bpb][:, :], ALU.add, ALU.mult, [("tg", 1)], [BK(bpb), ("tg", 3)])
                tt(mT[:, ot, cs], tg[2], tg[3], ALU.add, [("tg", 2), ("tg", 3)], [("mT", ot, bi)])
        P.barrier()

        t = 0
        NX2 = 4
        xt2 = [V32(t + i * 4096, D) for i in range(NX2)]; t += NX2 * 4096
        xn = [V32(t + i * 4096, D) for i in range(2)]; t += 8192
        ot_ = [V32(t + i * 4096, D) for i in range(3)]; t += 12288
        junk2 = V16(t, D); t += 2048
        assert t <= R_CONST
        def fin4(tk):
            bsel = tk % 2
            o3 = tk % 3
            stt(ot_[o3], xn[bsel], rstd[:, tk:tk + 1], gfin, ALU.mult, ALU.mult, [("xn", bsel, 0), ("xn", bsel, 1), ("rstd2", tk)], [("ot", o3)])
            P.dma("pool", (lambda tk=tk, o3=o3: lambda e: e.dma_start(out=out[tk * 128:(tk + 1) * 128, :], in_=ot_[o3]))(),
                  reads=[("ot", o3)], writes=[("outd", tk)])

        for tk in range(16):
            bsel = tk % 2
            x4 = tk % NX2
            ld(xt2[x4], xloc[OWN + tk * 128:OWN + (tk + 1) * 128, :], [("xt2", x4)])
            for hf in range(2):
                b = nb()
                mm_group(b, banks[b][:, :], [(mT[:, kt, tk * 128:(tk + 1) * 128], wob[:, kt, hf * 512:(hf + 1) * 512]) for kt in range(8)],
                         [("wob", kt) for kt in range(8)])
                tt(xn[bsel][:, hf * 512:(hf + 1) * 512], banks[b][:, :], xt2[x4][:, hf * 512:(hf + 1) * 512], ALU.add,
                   [("xt2", x4)], [BK(b), ("xn", bsel, hf)])
            act(junk2, xn[bsel], AF.Square, [("xn", bsel, 0), ("xn", bsel, 1)], ["junk2", ("ss2", tk)], accum=ss[:, tk:tk + 1])
            act(lnv[:, tk:tk + 1], ss[:, tk:tk + 1], AF.Ln, [("ss2", tk)], [("lnv2", tk)], bias=cols[:, 1:2], scale=1.0 / D)
            act(rstd[:, tk:tk + 1], lnv[:, tk:tk + 1], AF.Exp, [("lnv2", tk)], [("rstd2", tk)], scale=-0.5)
            if tk >= 1:
                fin4(tk - 1)
        fin4(15)
        P.final_wait("sp")
        P.emit(nc, st)
    return nc


def _tile_cols(w, ncol):
    K, C = w.shape
    return np.ascontiguousarray(w.reshape(8, 128, C // ncol, ncol).transpose(2, 1, 0, 3).reshape(C // ncol, 128, 8 * ncol))


def _col8(v):
    return np.ascontiguousarray(v.reshape(8, 128).T)


def _bias_tables(rpb, half):
    H = 16
    kp, kc = np.meshgrid(np.arange(2), np.arange(64), indexing="ij")
    kp = kp.reshape(128, 1); kc = kc.reshape(128, 1)
    j, s, qc = np.meshgrid(np.arange(5), np.arange(2), np.arange(64), indexing="ij")
    j = j.reshape(1, 640); s = s.reshape(1, 640); qc = qc.reshape(1, 640)
    out = np.full((8, 128, 3, 2, 640), NEG, np.float32)
    for var, pi in ((0, 0), (1, 14), (2, 15)):
        r_loc = 32 + 2 * pi
        R0 = min(max(r_loc - 4, 0), 54)
        kr_l = R0 + 2 * j + kp
        qr_l = r_loc + s + 0 * kp
        kc_l = kc + 0 * j
        qc_l = qc + 0 * kp
        if half == 1:
            kr, qr, kcg, qcg = kr_l, qr_l, kc_l, qc_l
        else:
            kr, qr, kcg, qcg = 63 - kr_l, 63 - qr_l, 63 - kc_l, 63 - qc_l
        r0 = np.clip(qr - 4, 0, 56)
        c0 = np.clip(qcg - 8, 0, 48)
        valid = (kr >= r0) & (kr <= r0 + 7) & (kcg >= c0) & (kcg <= c0 + 15) & (kr >= 0) & (kr <= 63)
        dr = np.clip(kr - qr + 7, 0, 14)
        dc = np.clip(kcg - qcg + 15, 0, 30)
        for h in range(H):
            vals = rpb[h][dr, dc]
            out[h // 2, :, var, h % 2, :] = np.where(valid, vals, np.float32(NEG))
    out = out.reshape(8, 128, 3, 2, 5, 128).transpose(0, 5, 2, 3, 4, 1)
    return np.ascontiguousarray(out.reshape(8, 128, 3 * 2 * 640))


_NC_CACHE = {}


def _prep_core(inp, b, half):
    f = np.float32
    xb = inp["x"][b]
    if half == 1:
        xloc = xb
    else:
        xloc = xb[::-1]
    m = {}
    m["xloc"] = np.ascontiguousarray(xloc, dtype=f)
    m["ccol"] = _col8(inp["c"][b])
    m["wc_t"] = _tile_cols(inp["w_c"][0], 256)
    m["bc_rep"] = np.ascontiguousarray(np.broadcast_to(inp["b_c"][0][None, :], (128, 3072)))
    m["gpre_rep"] = np.ascontiguousarray(np.broadcast_to(inp["g_pre"][0][None, :], (128, D)))
    m["gfin_rep"] = np.ascontiguousarray(np.broadcast_to(inp["g_final"][None, :], (128, D)))
    m["win_t"] = _tile_cols(inp["w_in"][0], 128)
    m["wv_t"] = _tile_cols(inp["w_in"][0][:, 4096:5120], 256)
    m["wpa_t"] = _tile_cols(inp["w_pa"][0], 128)
    m["wpb_t"] = _tile_cols(inp["w_pb"][0], 128)
    m["wo_t"] = np.ascontiguousarray(inp["w_o"][0].reshape(8, 128, D).transpose(1, 0, 2).reshape(128, 8 * D))
    cw = inp["conv_w"][0]
    wloc = np.zeros((5, D), f)
    if half == 1:
        wloc[0:4] = cw
    else:
        wloc[1:5] = cw[::-1]
    dcv = np.zeros((128, 8, 5, 128), f)
    idx = np.arange(128)
    for ct in range(8):
        for o_ in range(5):
            dcv[idx, ct, o_, idx] = wloc[o_, ct * 128 + idx]
    m["dconv"] = dcv.reshape(128, 8 * 5 * 128)
    if half == 1:
        gates = (inp["w_r_f"][0], inp["w_i_f"][0], inp["w_r_b"][0], inp["w_i_b"][0])
        vecs = (inp["conv_b"][0], inp["b_r_f"][0], inp["b_i_f"][0], inp["b_r_b"][0], inp["b_i_b"][0], inp["lam_f"][0], inp["lam_b"][0])
    else:
        gates = (inp["w_r_b"][0], inp["w_i_b"][0], inp["w_r_f"][0], inp["w_i_f"][0])
        vecs = (inp["conv_b"][0], inp["b_r_b"][0], inp["b_i_b"][0], inp["b_r_f"][0], inp["b_i_f"][0], inp["lam_b"][0], inp["lam_f"][0])
    gm = np.zeros((128, 8, 4, 128), f)
    for ct in range(8):
        for g in range(4):
            for hb_ in range(2):
                gm[hb_ * 64:(hb_ + 1) * 64, ct, g, hb_ * 64:(hb_ + 1) * 64] = gates[g][2 * ct + hb_]
    m["gmat"] = gm.reshape(128, 8 * 4 * 128)
    m["chvec"] = np.ascontiguousarray(np.stack([_col8(v) for v in vecs], axis=1).reshape(128, 56))
    m["bt"] = _bias_tables(inp["rpb"][0], half)
    m["ident"] = np.eye(128, dtype=f)
    return m


def kernel(**inputs):
    inp = {k: np.asarray(v, dtype=np.float32) for k, v in inputs.items()}
    if "nc" not in _NC_CACHE:
        _NC_CACHE["nc"] = build_nc()
    nc = _NC_CACHE["nc"]
    in_maps = []
    for core in range(8):
        in_maps.append(_prep_core(inp, core // 2, core % 2))
    res = run_bass_kernel_spmd(nc, in_maps, core_ids=list(range(8)))
    outp = np.zeros((4, NT, D), np.float32)
    for core in range(8):
        b, half = core // 2, core % 2
        o = np.asarray(res.results[core]["out"], dtype=np.float32)
        if half == 1:
            outp[b, OWN:] = o
        else:
            outp[b, 0:OWN] = o[::-1]
    return outp
```

```python
import numpy as np
from contextlib import ExitStack
import concourse.bass as bass
import concourse.mybir as mybir
from concourse.bass_utils import run_bass_kernel_spmd

F32 = mybir.dt.float32
BF16 = mybir.dt.bfloat16
AF = mybir.ActivationFunctionType
ALU = mybir.AluOpType

COMPUTE = ("pe", "act", "dve", "pool")
QUEUES = ("sp",) + COMPUTE


class Planner:
    def __init__(self, n_dma_sems=24):
        self.ops = {e: [] for e in QUEUES}
        self.count = {e: 0 for e in COMPUTE}
        self.waited = {e: {} for e in QUEUES}
        self.state = {}
        self.n_dma = n_dma_sems
        self.dma_i = 0
        self.dma_q = [0, 0]
        self.dma_cnt = [0] * n_dma_sems
        self.dma_last = [None] * n_dma_sems
        self.all_tickets = {}
        self.pe_mode = None
        self.pe_unsig = False

    def _deps(self, eng, reads, writes, extra=()):
        deps = {}

        def add(t):
            if t is None:
                return
            k, v = t
            if deps.get(k, 0) < v:
                deps[k] = v

        for k in reads:
            st = self.state.get(k)
            if st:
                add(st[0])
        for k in writes:
            st = self.state.get(k)
            if st:
                add(st[0])
                for t in st[1]:
                    add(t)
        for t in extra:
            add(t)
        waits = []
        for k, v in deps.items():
            if eng == "pe" and k == "pe":
                continue
            if self.waited[eng].get(k, 0) >= v:
                continue
            self.waited[eng][k] = v
            waits.append((k, v))
        return waits

    def _commit(self, ticket, reads, writes):
        for k in reads:
            st = self.state.setdefault(k, [None, []])
            st[1].append(ticket)
        for k in writes:
            self.state[k] = [ticket, []]
        self.all_tickets[ticket[0]] = max(self.all_tickets.get(ticket[0], 0), ticket[1])

    def op(self, eng, fn, reads=(), writes=(), signal=True, mode="mm"):
        waits = self._deps(eng, reads, writes)
        if eng == "pe":
            if self.pe_mode not in (None, mode):
                assert not self.pe_unsig
                v = self.count["pe"]
                if self.waited["pe"].get("pe", 0) < v:
                    self.waited["pe"]["pe"] = v
                    waits.append(("pe", v))
            self.pe_mode = mode
            self.pe_unsig = not signal
        if signal:
            self.count[eng] += 1
            ticket = (eng, self.count[eng])
            inc = (eng, 1)
        else:
            ticket = (eng, self.count[eng] + 1)
            inc = None
        self._commit(ticket, reads, writes)
        self.ops[eng].append((waits, fn, inc))
        return ticket

    def dma(self, queue, fn, reads=(), writes=()):
        half = self.n_dma // 2
        qi = 1 if queue == "pool" else 0
        i = qi * half + self.dma_q[qi] % half
        self.dma_q[qi] += 1
        key = ("dma", i)
        extra = [self.dma_last[i]] if self.dma_last[i] else []
        waits = self._deps(queue, reads, writes, extra)
        self.dma_cnt[i] += 16
        ticket = (key, self.dma_cnt[i])
        self.dma_last[i] = ticket
        self._commit(ticket, reads, writes)
        self.ops[queue].append((waits, fn, (key, 16)))
        return ticket

    def barrier(self):
        assert not self.pe_unsig
        for eng in QUEUES:
            waits = []
            for k, v in self.all_tickets.items():
                if self.waited[eng].get(k, 0) >= v:
                    continue
                self.waited[eng][k] = v
                waits.append((k, v))
            if waits:
                self.ops[eng].append((waits, None, None))
        self.state = {}

    def final_wait(self, eng="sp"):
        waits = []
        for k, v in self.all_tickets.items():
            if self.waited[eng].get(k, 0) >= v:
                continue
            self.waited[eng][k] = v
            waits.append((k, v))
        self.ops[eng].append((waits, None, None))

    def emit(self, nc, stack):
        sems = {}
        for e in COMPUTE:
            sems[e] = stack.enter_context(nc.semaphore("s_" + e))
        for i in range(self.n_dma):
            sems[("dma", i)] = stack.enter_context(nc.semaphore("s_dma%d" % i))
        block = stack.enter_context(nc.Block())

        def run(engname):
            def body(eng):
                for waits, fn, inc in self.ops[engname]:
                    for k, v in waits:
                        eng.wait_ge(sems[k], v)
                    if fn is None:
                        continue
                    ins = fn(eng)
                    if inc is not None:
                        ins.then_inc(sems[inc[0]], inc[1])
            return body

        block.sync(run("sp"))
        block.tensor(run("pe"))
        block.scalar(run("act"))
        block.vector(run("dve"))
        block.gpsimd(run("pool"))


D = 1024
NT = 4096
OWN = 2048
KV0 = 1792
NKV = NT - KV0
PS0 = 1920
NEG = -30000.0
EPS = 1e-6
ARENA_BYTES = 204 * 1024


def build_nc():
    nc = bass.Bass("TRN2", target_bir_lowering=False)

    def din(name, shape):
        return nc.dram_tensor(name, list(shape), F32, kind="ExternalInput").ap()

    xloc = din("xloc", [NT, D])
    ccol = din("ccol", [128, 8])
    wc_t = din("wc_t", [12, 128, 8 * 256])
    bc_rep = din("bc_rep", [128, 3072])
    gpre_rep = din("gpre_rep", [128, D])
    gfin_rep = din("gfin_rep", [128, D])
    win_t = din("win_t", [64, 128, 8 * 128])
    wv_t = din("wv_t", [4, 128, 8 * 256])
    wpa_t = din("wpa_t", [8, 128, 8 * 128])
    wpb_t = din("wpb_t", [8, 128, 8 * 128])
    wo_t = din("wo_t", [128, 8 * D])
    dconv = din("dconv", [128, 8 * 5 * 128])
    gmat = din("gmat", [128, 8 * 4 * 128])
    chvec = din("chvec", [128, 7 * 8])
    bt = din("bt", [8, 128, 3 * 2 * 640])
    ident = din("ident", [128, 128])
    out = nc.dram_tensor("out", [OWN, D], F32, kind="ExternalOutput").ap()

    P = Planner()
    with ExitStack() as st:
        arena = st.enter_context(nc.sbuf_tensor("arena", [128, ARENA_BYTES // 2], BF16))
        a16 = arena
        a32 = arena.bitcast(F32)
        psall = st.enter_context(nc.psum_tensor("psall", [128, 8 * 512], F32))
        psall16 = psall.bitcast(BF16)
        banks = [psall[:, i * 512:(i + 1) * 512] for i in range(8)]
        banks16 = [psall16[:, i * 1024:(i + 1) * 1024] for i in range(8)]

        def V16(off, n):
            assert off % 2 == 0 and off + 2 * n <= ARENA_BYTES, (off, n)
            return a16[:, off // 2: off // 2 + n]

        def V32(off, n):
            assert off % 4 == 0 and off + 4 * n <= ARENA_BYTES, (off, n)
            return a32[:, off // 4: off // 4 + n]

        def r3(ap, a):
            return ap.rearrange("p (a b) -> p a b", a=a)

        R_HOWN = 0
        R_YA = R_HOWN + 8 * NKV * 2
        R_YB = R_YA + 8 * OWN * 2
        R_CONST = R_YB + 8 * OWN * 2
        o = R_CONST
        ident_b = V16(o, 128); o += 256
        ones_b = V16(o, 128); o += 256
        chv = r3(V32(o, 56), 7); o += 224
        der = r3(V32(o, 80), 10); o += 320
        cols = V32(o, 8); o += 32
        carry = V32(o, 8); o += 32
        ss = V32(o, 32); o += 128
        lnv = V32(o, 32); o += 128
        rstd = V32(o, 32); o += 128
        dconv_b = V16(o, 5120); o += 10240
        gmat_b = V16(o, 4096); o += 8192
        G_bc = V32(o, D); o += 4096
        gfin = V32(o, D); o += 4096
        R_TMP = (o + 63) // 64 * 64
        TMP_BYTES = ARENA_BYTES - R_TMP

        hT_own = r3(V16(R_HOWN, 8 * NKV), 8)
        yaT = r3(V16(R_YA, 8 * OWN), 8)
        ybT = r3(V16(R_YB, 8 * OWN), 8)
        hT_oth = r3(V16(R_YB, 8 * OWN), 8)

        bank_rr = [0]

        def nb():
            b = bank_rr[0] % 8
            bank_rr[0] += 1
            return b

        def nb2():
            if bank_rr[0] % 2:
                bank_rr[0] += 1
            b = bank_rr[0] % 8
            bank_rr[0] += 2
            return b

        def BK(b):
            return ("ps", b)

        def mm_group(bank, out_ap, pairs, reads, mode="mm"):
            n = len(pairs)
            for i, (l, r) in enumerate(pairs):
                P.op("pe", (lambda l=l, r=r, s=(i == 0), e_=(i == n - 1):
                            lambda e: e.matmul(out_ap, l, r, start=s, stop=e_))(),
                     reads=reads, writes=[BK(bank)], signal=(i == n - 1), mode=mode)

        def act(out_ap, in_ap, func, reads, writes, bias=None, scale=None, accum=None):
            kw = {}
            if bias is not None:
                kw["bias"] = bias
            if scale is not None:
                kw["scale"] = scale
            if accum is not None:
                kw["accum_out"] = accum
            P.op("act", lambda e: e.activation(out_ap, in_ap, func, **kw), reads=reads, writes=writes)

        def stt(out_ap, in0, scalar, in1, op0, op1, reads, writes, eng="dve"):
            P.op(eng, lambda e: e.scalar_tensor_tensor(out_ap, in0, scalar, in1, op0, op1), reads=reads, writes=writes)

        def tt(out_ap, in0, in1, op, reads, writes, eng="dve"):
            P.op(eng, lambda e: e.tensor_tensor(out_ap, in0, in1, op), reads=reads, writes=writes)

        def ts(out_ap, in0, s1, s2, op0, op1, reads, writes, eng="dve"):
            P.op(eng, lambda e: e.tensor_scalar(out_ap, in0, s1, s2, op0, op1), reads=reads, writes=writes)

        def cp(out_ap, in_ap, reads, writes, eng="dve"):
            P.op(eng, lambda e: e.tensor_copy(out_ap, in_ap), reads=reads, writes=writes)

        def ms(ap, val, writes, eng="dve"):
            P.op(eng, lambda e: e.memset(ap, val), writes=writes)

        def ld(out_ap, in_ap, writes, q="sp", reads=()):
            P.dma(q, lambda e: e.dma_start(out=out_ap, in_=in_ap), reads=reads, writes=writes)

        def ldc(out_ap, in_ap, writes, reads=()):
            P.dma("pool", lambda e: e.dma_start(out=out_ap, in_=in_ap, max_dma_last_dim=4096), reads=reads, writes=writes)

        t = R_TMP
        NWS = 3
        wcs = [V32(t + i * 8192, 2048) for i in range(NWS)]; t += NWS * 8192
        wcb = [r3(V16(t + i * 4096, 2048), 8) for i in range(2)]; t += 8192
        sc_rep = r3(V16(t, 1024), 8); t += 2048
        bcr = V32(t, 3072); t += 12288
        gpr = V32(t, D); t += 4096
        cc = V32(t, 8); t += 32
        tz0 = V32(t, 8); t += 32
        s2 = V32(t, 8); t += 32
        ev = V32(t, 16); t += 64
        assert t - R_TMP <= TMP_BYTES
        mod_bc = V32(R_YA, 3072)
        A_bc = V32(R_YA + 12288, D)
        B_bc = mod_bc[:, 0:D]

        ld(cc, ccol, ["cc"])
        ld(chv.rearrange("p a b -> p (a b)"), chvec, ["chv"])
        for nbk in range(NWS):
            ld(wcs[nbk], wc_t[nbk], [("wcs", nbk)], q="sp")
        ld(bcr, bc_rep, ["bcr"])
        ld(gpr, gpre_rep, ["gpr"])
        ldc(ident_b, ident, ["ident"])
        ms(ones_b, 1.0, ["ones"])
        ms(cols[:, 0:1], 1.0, ["cols"])
        ms(cols[:, 1:2], EPS, ["cols"])
        ms(carry, 0.0, ["carry"])
        dcv = r3(dconv_b, 40)
        gmv = r3(gmat_b, 32)

        act(tz0, cc, AF.Tanh, ["cc"], ["tz0"], scale=0.5)
        stt(s2, tz0, 1.0, cc, ALU.add, ALU.mult, ["tz0", "cc"], ["s2"])
        for kt in range(8):
            ts(sc_rep[:, kt, :], ones_b, s2[:, kt:kt + 1], 0.5, ALU.mult, ALU.mult, ["ones", "s2"], [("scr", kt)])
        for nbk in range(12):
            w = wcb[nbk % 2]
            ws_ = nbk % NWS
            act(w.rearrange("p a b -> p (a b)"), wcs[ws_], AF.Copy, [("wcs", ws_)], [("wcb", nbk % 2)])
            if nbk + NWS < 12:
                ld(wcs[ws_], wc_t[nbk + NWS], [("wcs", ws_)], q="sp")
            b = nb()
            mm_group(b, banks[b][:, 0:256], [(sc_rep[:, kt, :], w[:, kt, :]) for kt in range(8)],
                     [("wcb", nbk % 2)] + [("scr", kt) for kt in range(8)])
            tt(mod_bc[:, nbk * 256:(nbk + 1) * 256], banks[b][:, 0:256], bcr[:, nbk * 256:(nbk + 1) * 256], ALU.add,
               ["bcr"], [BK(b), ("mod", nbk)])
        stt(A_bc, mod_bc[:, D:2 * D], 1.0, gpr, ALU.add, ALU.mult, [("mod", k) for k in range(4, 8)] + ["gpr"], ["A_bc"])
        cp(G_bc, mod_bc[:, 2 * D:3 * D], [("mod", k) for k in range(8, 12)], ["G_bc"])
        ld(gfin, gfin_rep, ["gfin"])
        for g in range(4):
            ts(der[:, g, :], chv[:, 1 + g, :], 0.5, None, ALU.mult, ALU.bypass, ["chv"], [("der", g)])
        act(ev, chv[:, 5:7, :].rearrange("p a b -> p (a b)"), AF.Exp, ["chv"], ["ev"], scale=-1.0)
        act(ev, ev, AF.Ln, ["ev", "cols"], ["ev"], bias=cols[:, 0:1], scale=1.0)
        for q in range(2):
            ts(der[:, 4 + 2 * q, :], ev[:, q * 8:(q + 1) * 8], -8.0, None, ALU.mult, ALU.bypass, ["ev"], [("der", 4 + 2 * q)])
            ts(der[:, 5 + 2 * q, :], ev[:, q * 8:(q + 1) * 8], -4.0, None, ALU.mult, ALU.bypass, ["ev"], [("der", 5 + 2 * q)])

        NXT = 4
        xt = [V32(t + i * 4096, D) for i in range(NXT)]; t += NXT * 4096
        junk = V16(t, D); t += 2048
        t1 = V32(t, D); t += 4096
        hb = [V16(t + i * 2048, D) for i in range(2)]; t += 4096
        assert t - R_TMP <= TMP_BYTES
        p1bank = {}

        def p1A(tk):
            x3 = tk % NXT
            ld(xt[x3], xloc[tk * 128:(tk + 1) * 128, :], [("xt", x3)])
            act(junk, xt[x3], AF.Square, [("xt", x3)], ["junk", ("ss", tk)], accum=ss[:, tk:tk + 1])
            act(lnv[:, tk:tk + 1], ss[:, tk:tk + 1], AF.Ln, [("ss", tk), "cols"], [("lnv", tk)], bias=cols[:, 1:2], scale=1.0 / D)
            act(rstd[:, tk:tk + 1], lnv[:, tk:tk + 1], AF.Exp, [("lnv", tk)], [("rstd", tk)], scale=-0.5)

        def p1B(tk):
            x3 = tk % NXT
            bsel = tk % 2
            stt(t1, xt[x3], rstd[:, tk:tk + 1], A_bc, ALU.mult, ALU.mult, [("xt", x3), ("rstd", tk), "A_bc"], ["t1"])
            tt(hb[bsel], t1, B_bc, ALU.add, ["t1"] + [("mod", k) for k in range(4)], [("hb", bsel)])
            b = nb()
            p1bank[tk] = b
            for kt in range(8):
                P.op("pe", (lambda kt=kt, b=b, bsel=bsel: lambda e: e.transpose(
                    banks16[b][:, kt * 128:(kt + 1) * 128], hb[bsel][:, kt * 128:(kt + 1) * 128], ident_b))(),
                    reads=[("hb", bsel), "ident"], writes=[BK(b)], signal=(kt == 7), mode="tr")

        def p1C(tk):
            b = p1bank[tk]
            src = r3(banks16[b][:, :], 8)
            if tk < 16:
                act(hT_oth[:, :, tk * 128:(tk + 1) * 128], src, AF.Copy, [], [BK(b), ("hoth", tk)])
            if tk >= 16:
                j = tk - 14
                act(hT_own[:, :, j * 128:(j + 1) * 128], src, AF.Copy, [], [BK(b), ("hown", j)])
            elif tk >= 14:
                j = tk - 14
                cp(hT_own[:, :, j * 128:(j + 1) * 128], src, [], [BK(b), ("hown", j)])

        for s_ in range(34):
            if s_ < 32:
                p1A(s_)
            if s_ == 10:
                ldc(dconv_b, dconv, ["dconv"], reads=[("xt", 10 % NXT)])
                ldc(gmat_b, gmat, ["gmat"], reads=[("xt", 10 % NXT)])
            if 0 <= s_ - 1 < 32:
                p1B(s_ - 1)
            if 0 <= s_ - 2 < 32:
                p1C(s_ - 2)
        P.barrier()

        def branchA(part):
            own = (part == 1)
            t = R_TMP
            WP = 2176 if own else 2048
            SEG = []
            for q_ in range(2 if own else 1):
                wseg = WP if q_ == 0 else OWN
                SEG.append(dict(TR=V32(t, wseg), TI=V32(t + wseg * 4, wseg), A2=V32(t + wseg * 8, wseg), U=V32(t + wseg * 12, wseg)))
                t += wseg * 16
            if not own:
                SEG0_alt = dict(TR=V32(t, WP), TI=V32(t + WP * 4, WP)); t += WP * 8
                wA2 = [r3(V16(t + i * 2048, 1024), 8) for i in range(2)]; t += 4096
                wA = [wA2[0], None, wA2[1], None]
            else:
                wA = [r3(V16(t + i * 2048, 1024), 8) for i in range(4)]; t += 8192
            if own:
                assert t - R_TMP <= TMP_BYTES, t - R_TMP
                t = R_YB
                lim = R_YB + 8 * OWN * 2
            else:
                lim = R_TMP + TMP_BYTES
            NX = NKV if own else 2048
            xa_bf = V16(t, NX + 4); t += (NX + 4) * 2
            t = (t + 3) // 4 * 4
            xc_f = V32(t, NX); t += NX * 4
            if not own:
                xc_f2 = V32(t, NX); t += NX * 4
            xc_b = V16(t, NX); t += NX * 2
            t = (t + 3) // 4 * 4
            if own:
                HP = V32(t, 2176); t += 2176 * 4
                NTZ = 5
                tzt = [V32(t + i * 1024, 256) for i in range(NTZ)]; t += NTZ * 1024
            assert t <= lim, (t, lim)
            hsrc = hT_own if own else hT_oth
            xblocks = [(i * 512, 512) for i in range(NX // 512)] + ([(2048, 256)] if own else [])
            nxb = len(xblocks)
            xblocks_i = [(i, c0, w_) for i, (c0, w_) in enumerate(xblocks)]
            ms(xa_bf[:, 0:2], 0.0, [("xa", "padl")])
            ms(xa_bf[:, NX + 2:NX + 4], 0.0, [("xa", "padr")])

            def loadw(ct):
                ldc(wA[(ct % 2) * 2].rearrange("p a b -> p (a b)"), win_t[ct], [("wA", (ct % 2) * 2)])
                if own:
                    ldc(wA[(ct % 2) * 2 + 1].rearrange("p a b -> p (a b)"), win_t[8 + ct], [("wA", (ct % 2) * 2 + 1)])

            def segbuf(q_, ct):
                S_ = SEG[q_]
                if (not own) and ct % 2 == 1:
                    S_ = dict(S_, TR=SEG0_alt["TR"], TI=SEG0_alt["TI"])
                return S_

            def sk(q_, gi, ct):
                if (not own) and gi in (0, 1):
                    return ("seg", q_, gi, ct % 2)
                return ("seg", q_, gi)

            loadw(0)
            segs = [(0, 0, (PS0 - KV0) if own else 0, 2176 if own else PS0)]
            if own:
                segs.append((1, 2, 256, OWN))
            def XCF(ct):
                return xc_f if (own or ct % 2 == 0) else xc_f2

            def xfkey(ct, bi):
                return ("xcf", bi) if own else ("xcf", ct % 2, bi)

            def stage_F(ct):
                wx = wA[(ct % 2) * 2]
                wz = wA[(ct % 2) * 2 + 1]
                if ct + 1 < 8:
                    loadw(ct + 1)
                pairs = [xblocks_i[i:i + 2] for i in range(0, nxb, 2)]
                for pr in pairs:
                    b = nb2()
                    for k_, (bi, c0, w_) in enumerate(pr):
                        mm_group(b + k_, banks[b + k_][:, 0:w_], [(wx[:, kt, :], hsrc[:, kt, c0:c0 + w_]) for kt in range(8)],
                                 [("wA", (ct % 2) * 2)])
                    c0 = pr[0][1]
                    W = sum(p_[2] for p_ in pr)
                    if own:
                        act(xa_bf[:, 2 + c0:2 + c0 + W], psall[:, b * 512:b * 512 + W], AF.Copy, [],
                            [BK(b + k_) for k_ in range(len(pr))] + [("xa", p_[0]) for p_ in pr])
                    else:
                        cp(xa_bf[:, 2 + c0:2 + c0 + W], psall[:, b * 512:b * 512 + W], [],
                           [BK(b + k_) for k_ in range(len(pr))] + [("xa", p_[0]) for p_ in pr])
                for pr in pairs:
                    b = nb2()
                    for k_, (bi, c0, w_) in enumerate(pr):
                        rk = [("xa", k) for k in range(max(0, bi - 1), min(nxb, bi + 2))] + [("xa", "padl"), ("xa", "padr"), "dconv"]
                        mm_group(b + k_, banks[b + k_][:, 0:w_], [(dcv[:, ct * 5 + o_, :], xa_bf[:, c0 + o_:c0 + o_ + w_]) for o_ in range(5)], rk)
                    c0 = pr[0][1]
                    W = sum(p_[2] for p_ in pr)
                    bks = [BK(b + k_) for k_ in range(len(pr))]
                    src_ = psall[:, b * 512:b * 512 + W]
                    if own:
                        act(xc_b[:, c0:c0 + W], src_, AF.Identity, ["chv"], bks + [("xcb", p_[0]) for p_ in pr], bias=chv[:, 0, ct:ct + 1])
                        act(XCF(ct)[:, c0:c0 + W], src_, AF.Identity, ["chv"], bks + [xfkey(ct, p_[0]) for p_ in pr], bias=chv[:, 0, ct:ct + 1])
                    else:
                        ts(XCF(ct)[:, c0:c0 + W], src_, chv[:, 0, ct:ct + 1], None, ALU.add, ALU.bypass, ["chv"], bks + [xfkey(ct, p_[0]) for p_ in pr])
                        cp(xc_b[:, c0:c0 + W], XCF(ct)[:, c0:c0 + W], [xfkey(ct, p_[0]) for p_ in pr], [("xcb", p_[0]) for p_ in pr])
                if own:
                    for bi in range(8):
                        c0 = 256 + bi * 256
                        b = nb()
                        tz = tzt[(ct * 8 + bi) % NTZ]
                        tk_ = ("tzt", (ct * 8 + bi) % NTZ)
                        mm_group(b, banks[b][:, 0:256], [(wz[:, kt, :], hT_own[:, kt, c0:c0 + 256]) for kt in range(8)],
                                 [("wA", (ct % 2) * 2 + 1)])
                        act(tz, banks[b][:, 0:256], AF.Tanh, [], [BK(b), tk_], scale=0.5)
                        stt(yaT[:, ct, bi * 256:(bi + 1) * 256], tz, 1.0, banks[b][:, 0:256], ALU.add, ALU.mult, [tk_], [BK(b), ("sza", ct, bi)])

            def stage_G1(ct):
                for (q_, g0, s0, wd) in segs:
                    S_ = segbuf(q_, ct)
                    c = 0
                    while c < wd:
                        w1 = min(512, wd - c)
                        w2 = min(512, wd - c - w1)
                        w_ = w1 + w2
                        xk = [("xcb", k) for k in range((s0 + c) // 512, min(nxb, (s0 + c + w_ - 1) // 512 + 1))]
                        for gi, dst in ((0, S_["TR"]), (1, S_["TI"])):
                            b = nb2()
                            mm_group(b, banks[b][:, 0:w1], [(gmv[:, ct * 4 + g0 + gi, :], xc_b[:, s0 + c:s0 + c + w1])], xk + ["gmat"])
                            bks = [BK(b)]
                            if w2 > 0:
                                mm_group(b + 1, banks[b + 1][:, 0:w2], [(gmv[:, ct * 4 + g0 + gi, :], xc_b[:, s0 + c + w1:s0 + c + w_])], xk + ["gmat"])
                                bks.append(BK(b + 1))
                            act(dst[:, c:c + w_], psall[:, b * 512:b * 512 + w_], AF.Tanh, [("der", g0 + gi)], bks + [sk(q_, gi, ct)],
                                bias=der[:, g0 + gi, ct:ct + 1], scale=0.5)
                        c += w_
                    hcP = der[:, 5 + g0, ct:ct + 1]
                    act(S_["TR"][:, 0:wd], S_["TR"][:, 0:wd], AF.Exp, [sk(q_, 0, ct), ("der", 5 + g0)], [sk(q_, 0, ct)], bias=hcP, scale=hcP)
                    tt(S_["A2"][:, 0:wd], S_["TR"][:, 0:wd], S_["TR"][:, 0:wd], ALU.mult, [sk(q_, 0, ct)], [("seg", q_, 2)], eng="pool")

            def stage_G2(ct):
                for (q_, g0, s0, wd) in segs:
                    S_ = segbuf(q_, ct)
                    act(S_["A2"][:, 0:wd], S_["A2"][:, 0:wd], AF.Sqrt, [("seg", q_, 2), "cols"], [("seg", q_, 2)], bias=cols[:, 0:1], scale=-1.0)

            def stage_D(ct):
                for (q_, g0, s0, wd) in segs:
                    S_ = segbuf(q_, ct)
                    TR, TI, A2, U = S_["TR"], S_["TI"], S_["A2"], S_["U"]
                    xfk = [xfkey(ct, k) for k in range(s0 // 512, min(nxb, (s0 + wd - 1) // 512 + 1))]
                    stt(U[:, 0:wd], TI[:, 0:wd], 1.0, XCF(ct)[:, s0:s0 + wd], ALU.add, ALU.mult, [sk(q_, 1, ct)] + xfk, [("seg", q_, 3)])
                    stt(U[:, 0:wd], U[:, 0:wd], 0.5, A2[:, 0:wd], ALU.mult, ALU.mult, [("seg", q_, 3), ("seg", q_, 2)], [("seg", q_, 3)])
                    if not own:
                        P.op("dve", (lambda TI=TI, TR=TR, U=U, wd=wd: lambda e: e.tensor_tensor_scan(TI[:, 0:wd], TR[:, 0:wd], U[:, 0:wd], 0.0, ALU.mult, ALU.add))(),
                             reads=[sk(q_, 0, ct), ("seg", q_, 3)], writes=[sk(q_, 1, ct)])
                        cp(carry[:, ct:ct + 1], TI[:, wd - 1:wd], [sk(q_, 1, ct)], [("carry", ct)])
                    elif q_ == 0:
                        P.op("dve", (lambda TR=TR, U=U, wd=wd, ct=ct: lambda e: e.tensor_tensor_scan(HP[:, 0:wd], TR[:, 0:wd], U[:, 0:wd], carry[:, ct:ct + 1], ALU.mult, ALU.add))(),
                             reads=[sk(q_, 0, ct), ("seg", q_, 3), ("carry", ct)], writes=["HP"])
                    else:
                        P.op("dve", (lambda A2=A2, TR=TR, U=U, wd=wd: lambda e: e.tensor_tensor_scan(A2[:, 0:wd][:, ::-1], TR[:, 0:wd][:, ::-1], U[:, 0:wd][:, ::-1], 0.0, ALU.mult, ALU.add))(),
                             reads=[("seg", q_, 0), ("seg", q_, 3)], writes=[("seg", q_, 2)])
                        tt(HP[:, 128:2176], HP[:, 128:2176], A2[:, 0:OWN], ALU.add, ["HP", ("seg", q_, 2)], ["HP"], eng="pool")
                        tt(yaT[:, ct, :], HP[:, 128:2176], yaT[:, ct, :], ALU.mult, ["HP"] + [("sza", ct, k) for k in range(8)], [("ya", ct)], eng="pool")

            stage_F(0)
            for ct in range(8):
                stage_G1(ct)
                if (not own) and ct + 1 < 8:
                    stage_F(ct + 1)
                stage_G2(ct)
                stage_D(ct)
                if own and ct + 1 < 8:
                    stage_F(ct + 1)
            P.barrier()

        branchA(0)
        branchA(1)

        t = R_TMP
        wvb = r3(V16(t, 2048), 8); t += 4096
        Vb = r3(V16(t, 18 * 256), 18); t += 9216
        wq = [r3(V16(t + i * 2048, 1024), 8) for i in range(3)]; t += 6144
        btb = V16(t, 3840); t += 7680
        qz = [V16(t + i * 4096, OWN) for i in range(2)]; t += 8192
        kT = V16(t, NKV); t += NKV * 2
        szb = V16(t, OWN); t += 4096
        Lg = [V32(t + i * 5120, 1280) for i in range(2)]; t += 10240
        PT = [V16(t + i * 2560, 1280) for i in range(3)]; t += 7680
        lnd = [V32(t + i * 512, 128) for i in range(2)]; t += 1024
        rden = [V32(t + i * 512, 128) for i in range(2)]; t += 1024
        tn = [V32(t + i * 512, 128) for i in range(2)]; t += 1024
        tzbs = [V32(t + i * 2048, 512) for i in range(2)]; t += 4096
        assert t - R_TMP <= TMP_BYTES, t - R_TMP
        ms(qz[0][64:128, :], 0.0, [("qzz", 0)])
        ms(qz[1][0:64, :], 0.0, [("qzz", 1)])
        kvblocks = [(i * 512, 512) for i in range(4)] + [(2048, 256)]
        SB = [[0, 1, 2], [3, 4, 5]]
        NDB = [6, 7]
        step = [0]

        def att_S(hp, hpl, pi):
            i = step[0]
            step[0] += 1
            sl = i % 2
            r_loc = 32 + 2 * pi
            R0 = min(max(r_loc - 4, 0), 54)
            koff = (R0 - 28) * 64
            qoff = pi * 128
            var = 0 if pi <= 13 else pi - 13
            kkeys = [("kT", k) for k in range(koff // 512, (koff + 639) // 512 + 1)]
            B = SB[sl]
            for e_ in range(2):
                for j in range(5):
                    if j < 4:
                        bb, oc = B[e_], j * 128
                    else:
                        bb, oc = B[2], e_ * 128
                    for kh in range(2):
                        P.op("pe", (lambda bb=bb, oc=oc, kh=kh, j=j, e_=e_: lambda e: e.matmul(
                            banks[bb][kh * 64:(kh + 1) * 64, oc:oc + 128],
                            kT[:, koff + 128 * j + 64 * kh:koff + 128 * j + 64 * (kh + 1)],
                            qz[e_][:, qoff:qoff + 128], start=True, stop=False))(),
                            reads=kkeys + [("qz", e_, pi // 4), ("qzz", e_)], writes=[BK(bb)], signal=False, mode="ct")
                        bcol = ((var * 2 + e_) * 5 + j) * 128 + kh * 64
                        P.op("pe", (lambda bb=bb, oc=oc, kh=kh, bcol=bcol: lambda e: e.matmul(
                            banks[bb][kh * 64:(kh + 1) * 64, oc:oc + 128],
                            btb[:, bcol:bcol + 64], ident_b, start=False, stop=True))(),
                            reads=["btb", "ident"], writes=[BK(bb)], signal=(kh == 1), mode="ct")
            return dict(hp=hp, hpl=hpl, pi=pi, sl=sl, tk0=koff // 128, qoff=qoff, p3=i % 3, var=var)

        def att_L(d):
            pass

        def att_X(d):
            sl, p3 = d["sl"], d["p3"]
            B = SB[sl]
            for e_ in range(2):
                act(PT[p3][:, e_ * 640:e_ * 640 + 512], banks[B[e_]][:, :], AF.Exp, [], [BK(B[e_]), ("PT", p3, e_)])
            act(r3(PT[p3], 2)[:, :, 512:640], r3(banks[B[2]][:, 0:256], 2), AF.Exp, [], [BK(B[2]), ("PT", p3, 2)])

        def att_ND(d):
            sl, p3, tk0, hpl = d["sl"], d["p3"], d["tk0"], d["hpl"]
            bn_ = NDB[sl]
            vk = [("V", tk0 + j) for j in range(5)]
            for which in range(2):
                for j in range(5):
                    for e_ in range(2):
                        hs = slice(e_ * 64, (e_ + 1) * 64)
                        rhs_ = PT[p3][:, e_ * 640 + j * 128:e_ * 640 + (j + 1) * 128]
                        if which == 0:
                            lhs_ = Vb[:, tk0 + j, hpl * 128 + e_ * 64:hpl * 128 + (e_ + 1) * 64]
                            out_ = banks[bn_][hs, 0:128]
                        else:
                            lhs_ = ones_b[:, 0:64]
                            out_ = banks[bn_][hs, 128:256]
                        P.op("pe", (lambda out_=out_, lhs_=lhs_, rhs_=rhs_, j=j: lambda e: e.matmul(
                            out_, lhs_, rhs_, start=(j == 0), stop=(j == 4)))(),
                            reads=[("PT", p3, 0), ("PT", p3, 1), ("PT", p3, 2), "ones"] + vk, writes=[BK(bn_)], signal=(j == 4 and e_ == 1), mode="ct")

        def att_R(d):
            sl = d["sl"]
            bn_ = NDB[sl]
            P.op("dve", (lambda sl=sl, bn_=bn_: lambda e: e.reciprocal(rden[sl], banks[bn_][:, 128:256]))(),
                 reads=[], writes=[BK(bn_), ("rden", sl)])

        def att_E(d):
            sl, hp, pi, qoff = d["sl"], d["hp"], d["pi"], d["qoff"]
            bn_ = NDB[sl]
            tt(tn[sl], banks[bn_][:, 0:128], rden[sl], ALU.mult, [("rden", sl)], [BK(bn_), ("tn", sl)])
            tt(ybT[:, hp, qoff:qoff + 128], tn[sl], szb[:, qoff:qoff + 128], ALU.mult,
               [("tn", sl), ("szb", pi // 4)], [("yb", hp, pi)])

        pend = []

        def att_flush():
            while pend:
                old_ = pend.pop(0)
                att_ND(old_)
                att_R(old_)
                att_E(old_)

        for hq in range(4):
            att_flush()
            ldc(wvb.rearrange("p a b -> p (a b)"), wv_t[hq], ["wvb"])
            for tk in range(18):
                b = nb()
                mm_group(b, banks[b][:, 0:256], [(hT_own[:, kt, tk * 128:(tk + 1) * 128], wvb[:, kt, :]) for kt in range(8)], ["wvb"])
                if tk % 2 == 0:
                    act(Vb[:, tk, :], banks[b][:, 0:256], AF.Copy, [], [BK(b), ("V", tk)])
                else:
                    cp(Vb[:, tk, :], banks[b][:, 0:256], [], [BK(b), ("V", tk)])
            for hpl in range(2):
                hp = hq * 2 + hpl
                for i, base in enumerate((16, 24, 40)):
                    ldc(wq[i].rearrange("p a b -> p (a b)"), win_t[base + hp], [("wq", i)])
                ldc(btb, bt[hp], ["btb"])
                for bi in range(4):
                    c0 = 256 + bi * 512
                    b = nb()
                    mm_group(b, banks[b][:, :], [(wq[0][:, kt, :], hT_own[:, kt, c0:c0 + 512]) for kt in range(8)], [("wq", 0)])
                    act(qz[0][0:64, bi * 512:(bi + 1) * 512], banks[b][0:64, :], AF.Copy, [], [BK(b), ("qz", 0, bi)], scale=0.125)
                    ts(qz[1][64:128, bi * 512:(bi + 1) * 512], banks[b][64:128, :], 0.125, None, ALU.mult, ALU.bypass, [], [BK(b), ("qz", 1, bi)])
                for bi, (c0, w_) in enumerate(kvblocks):
                    b = nb()
                    mm_group(b, banks[b][:, 0:w_], [(wq[1][:, kt, :], hT_own[:, kt, c0:c0 + w_]) for kt in range(8)], [("wq", 1)])
                    act(kT[:, c0:c0 + w_], banks[b][:, 0:w_], AF.Copy, [], [BK(b), ("kT", bi)])
                att_flush()
                for bi in range(4):
                    c0 = 256 + bi * 512
                    b = nb()
                    mm_group(b, banks[b][:, :], [(wq[2][:, kt, :], hT_own[:, kt, c0:c0 + 512]) for kt in range(8)], [("wq", 2)])
                    tzb = tzbs[bi % 2]
                    act(tzb, banks[b][:, :], AF.Tanh, [], [BK(b), ("tzb", bi % 2)], scale=0.5)
                    stt(szb[:, bi * 512:(bi + 1) * 512], tzb, 1.0, banks[b][:, :], ALU.add, ALU.mult, [("tzb", bi % 2)], [BK(b), ("szb", bi)])
                last_hp = (hq == 3 and hpl == 1)
                for pi in range(16 + (2 if last_hp else 0)):
                    cur = att_S(hp, hpl, pi) if pi < 16 else None
                    old_ = pend.pop(0) if len(pend) == 2 or (cur is None and pend) else None
                    if old_ is not None:
                        att_ND(old_)
                        att_R(old_)
                    if cur is not None:
                        att_L(cur)
                        att_X(cur)
                        pend.append(cur)
                    if old_ is not None:
                        att_E(old_)
        P.barrier()

        t = R_TMP
        mT = r3(V16(t, 8 * OWN), 8); t += 8 * OWN * 2
        w4 = [r3(V16(t + i * 2048, 1024), 8) for i in range(8)]; t += 16384
        tg = [V32(t + i * 2048, 512) for i in range(4)]; t += 8192
        wost = V32(t, D); t += 4096
        wob = r3(V16(t, 8 * D), 8); t += 16384
        assert t - R_TMP <= TMP_BYTES, t - R_TMP
        for kt in range(8):
            ld(wost, wo_t[:, kt * D:(kt + 1) * D], ["wost"])
            stt(wob[:, kt, :], wost, 0.25, G_bc, ALU.mult, ALU.mult, ["wost"], [("wob", kt)])
        for ot in range(8):
            wb = (ot % 2) * 4
            srcs = (wpa_t[ot], wpb_t[ot], win_t[48 + ot], win_t[56 + ot])
            for i in range(4):
                ldc(w4[wb + i].rearrange("p a b -> p (a b)"), srcs[i], [("w4", wb + i)])
            for bi in range(4):
                c0 = 256 + bi * 512
                cs = slice(bi * 512, (bi + 1) * 512)
                bga = nb()
                mm_group(bga, banks[bga][:, :], [(w4[wb + 2][:, kt, :], hT_own[:, kt, c0:c0 + 512]) for kt in range(8)], [("w4", wb + 2)])
                act(tg[0], banks[bga][:, :], AF.Tanh, [], [BK(bga), ("tg", 0)], scale=0.5)
                bgb = nb()
                mm_group(bgb, banks[bgb][:, :], [(w4[wb + 3][:, kt, :], hT_own[:, kt, c0:c0 + 512]) for kt in range(8)], [("w4", wb + 3)])
                act(tg[1], banks[bgb][:, :], AF.Tanh, [], [BK(bgb), ("tg", 1)], scale=0.5)
                bpa = nb()
                mm_group(bpa, banks[bpa][:, :], [(w4[wb + 0][:, kt, :], yaT[:, kt, cs]) for kt in range(8)], [("w4", wb + 0)])
                stt(tg[2], tg[0], 1.0, banks[bpa][:, :], ALU.add, ALU.mult, [("tg", 0)], [BK(bpa), ("tg", 2)])
                bpb = nb()
                mm_group(bpb, banks[bpb][:, :], [(w4[wb + 1][:, kt, :], ybT[:, kt, cs]) for kt in range(8)], [("w4", wb + 1)])
                stt(tg[3], tg[1], 1.0, banks[bpb][:, :], ALU.add, ALU.mult, [("tg", 1)], [BK(bpb), ("tg", 3)])
                tt(mT[:, ot, cs], tg[2], tg[3], ALU.add, [("tg", 2), ("tg", 3)], [("mT", ot, bi)])
        P.barrier()

        t = 0
        NX2 = 4
        xt2 = [V32(t + i * 4096, D) for i in range(NX2)]; t += NX2 * 4096
        xn = [V32(t + i * 4096, D) for i in range(2)]; t += 8192
        ot_ = [V32(t + i * 4096, D) for i in range(3)]; t += 12288
        junk2 = V16(t, D); t += 2048
        assert t <= R_CONST
        def fin4(tk):
            bsel = tk % 2
            o3 = tk % 3
            stt(ot_[o3], xn[bsel], rstd[:, tk:tk + 1], gfin, ALU.mult, ALU.mult, [("xn", bsel, 0), ("xn", bsel, 1), ("rstd2", tk)], [("ot", o3)])
            P.dma("pool", (lambda tk=tk, o3=o3: lambda e: e.dma_start(out=out[tk * 128:(tk + 1) * 128, :], in_=ot_[o3]))(),
                  reads=[("ot", o3)], writes=[("outd", tk)])

        for tk in range(16):
            bsel = tk % 2
            x4 = tk % NX2
            ld(xt2[x4], xloc[OWN + tk * 128:OWN + (tk + 1) * 128, :], [("xt2", x4)])
            for hf in range(2):
                b = nb()
                mm_group(b, banks[b][:, :], [(mT[:, kt, tk * 128:(tk + 1) * 128], wob[:, kt, hf * 512:(hf + 1) * 512]) for kt in range(8)],
                         [("wob", kt) for kt in range(8)])
                tt(xn[bsel][:, hf * 512:(hf + 1) * 512], banks[b][:, :], xt2[x4][:, hf * 512:(hf + 1) * 512], ALU.add,
                   [("xt2", x4)], [BK(b), ("xn", bsel, hf)])
            act(junk2, xn[bsel], AF.Square, [("xn", bsel, 0), ("xn", bsel, 1)], ["junk2", ("ss2", tk)], accum=ss[:, tk:tk + 1])
            act(lnv[:, tk:tk + 1], ss[:, tk:tk + 1], AF.Ln, [("ss2", tk)], [("lnv2", tk)], bias=cols[:, 1:2], scale=1.0 / D)
            act(rstd[:, tk:tk + 1], lnv[:, tk:tk + 1], AF.Exp, [("lnv2", tk)], [("rstd2", tk)], scale=-0.5)
            if tk >= 1:
                fin4(tk - 1)
        fin4(15)
        P.final_wait("sp")
        P.emit(nc, st)
    return nc


def _tile_cols(w, ncol):
    K, C = w.shape
    return np.ascontiguousarray(w.reshape(8, 128, C // ncol, ncol).transpose(2, 1, 0, 3).reshape(C // ncol, 128, 8 * ncol))


def _col8(v):
    return np.ascontiguousarray(v.reshape(8, 128).T)


def _bias_tables(rpb, half):
    H = 16
    kp, kc = np.meshgrid(np.arange(2), np.arange(64), indexing="ij")
    kp = kp.reshape(128, 1); kc = kc.reshape(128, 1)
    j, s, qc = np.meshgrid(np.arange(5), np.arange(2), np.arange(64), indexing="ij")
    j = j.reshape(1, 640); s = s.reshape(1, 640); qc = qc.reshape(1, 640)
    out = np.full((8, 128, 3, 2, 640), NEG, np.float32)
    for var, pi in ((0, 0), (1, 14), (2, 15)):
        r_loc = 32 + 2 * pi
        R0 = min(max(r_loc - 4, 0), 54)
        kr_l = R0 + 2 * j + kp
        qr_l = r_loc + s + 0 * kp
        kc_l = kc + 0 * j
        qc_l = qc + 0 * kp
        if half == 1:
            kr, qr, kcg, qcg = kr_l, qr_l, kc_l, qc_l
        else:
            kr, qr, kcg, qcg = 63 - kr_l, 63 - qr_l, 63 - kc_l, 63 - qc_l
        r0 = np.clip(qr - 4, 0, 56)
        c0 = np.clip(qcg - 8, 0, 48)
        valid = (kr >= r0) & (kr <= r0 + 7) & (kcg >= c0) & (kcg <= c0 + 15) & (kr >= 0) & (kr <= 63)
        dr = np.clip(kr - qr + 7, 0, 14)
        dc = np.clip(kcg - qcg + 15, 0, 30)
        for h in range(H):
            vals = rpb[h][dr, dc]
            out[h // 2, :, var, h % 2, :] = np.where(valid, vals, np.float32(NEG))
    out = out.reshape(8, 128, 3, 2, 5, 128).transpose(0, 5, 2, 3, 4, 1)
    return np.ascontiguousarray(out.reshape(8, 128, 3 * 2 * 640))


_NC_CACHE = {}


def _prep_core(inp, b, half):
    f = np.float32
    xb = inp["x"][b]
    if half == 1:
        xloc = xb
    else:
        xloc = xb[::-1]
    m = {}
    m["xloc"] = np.ascontiguousarray(xloc, dtype=f)
    m["ccol"] = _col8(inp["c"][b])
    m["wc_t"] = _tile_cols(inp["w_c"][0], 256)
    m["bc_rep"] = np.ascontiguousarray(np.broadcast_to(inp["b_c"][0][None, :], (128, 3072)))
    m["gpre_rep"] = np.ascontiguousarray(np.broadcast_to(inp["g_pre"][0][None, :], (128, D)))
    m["gfin_rep"] = np.ascontiguousarray(np.broadcast_to(inp["g_final"][None, :], (128, D)))
    m["win_t"] = _tile_cols(inp["w_in"][0], 128)
    m["wv_t"] = _tile_cols(inp["w_in"][0][:, 4096:5120], 256)
    m["wpa_t"] = _tile_cols(inp["w_pa"][0], 128)
    m["wpb_t"] = _tile_cols(inp["w_pb"][0], 128)
    m["wo_t"] = np.ascontiguousarray(inp["w_o"][0].reshape(8, 128, D).transpose(1, 0, 2).reshape(128, 8 * D))
    cw = inp["conv_w"][0]
    wloc = np.zeros((5, D), f)
    if half == 1:
        wloc[0:4] = cw
    else:
        wloc[1:5] = cw[::-1]
    dcv = np.zeros((128, 8, 5, 128), f)
    idx = np.arange(128)
    for ct in range(8):
        for o_ in range(5):
            dcv[idx, ct, o_, idx] = wloc[o_, ct * 128 + idx]
    m["dconv"] = dcv.reshape(128, 8 * 5 * 128)
    if half == 1:
        gates = (inp["w_r_f"][0], inp["w_i_f"][0], inp["w_r_b"][0], inp["w_i_b"][0])
        vecs = (inp["conv_b"][0], inp["b_r_f"][0], inp["b_i_f"][0], inp["b_r_b"][0], inp["b_i_b"][0], inp["lam_f"][0], inp["lam_b"][0])
    else:
        gates = (inp["w_r_b"][0], inp["w_i_b"][0], inp["w_r_f"][0], inp["w_i_f"][0])
        vecs = (inp["conv_b"][0], inp["b_r_b"][0], inp["b_i_b"][0], inp["b_r_f"][0], inp["b_i_f"][0], inp["lam_b"][0], inp["lam_f"][0])
    gm = np.zeros((128, 8, 4, 128), f)
    for ct in range(8):
        for g in range(4):
            for hb_ in range(2):
                gm[hb_ * 64:(hb_ + 1) * 64, ct, g, hb_ * 64:(hb_ + 1) * 64] = gates[g][2 * ct + hb_]
    m["gmat"] = gm.reshape(128, 8 * 4 * 128)
    m["chvec"] = np.ascontiguousarray(np.stack([_col8(v) for v in vecs], axis=1).reshape(128, 56))
    m["bt"] = _bias_tables(inp["rpb"][0], half)
    m["ident"] = np.eye(128, dtype=f)
    return m


def kernel(**inputs):
    inp = {k: np.asarray(v, dtype=np.float32) for k, v in inputs.items()}
    if "nc" not in _NC_CACHE:
        _NC_CACHE["nc"] = build_nc()
    nc = _NC_CACHE["nc"]
    in_maps = []
    for core in range(8):
        in_maps.append(_prep_core(inp, core // 2, core % 2))
    res = run_bass_kernel_spmd(nc, in_maps, core_ids=list(range(8)))
    outp = np.zeros((4, NT, D), np.float32)
    for core in range(8):
        b, half = core // 2, core % 2
        o = np.asarray(res.results[core]["out"], dtype=np.float32)
        if half == 1:
            outp[b, OWN:] = o
        else:
            outp[b, 0:OWN] = o[::-1]
    return outp
```
